# Optimizing a Trainium2 kernel written in Bass

```python
import jax, jax.numpy as jnp
from jax import lax
import numpy as np

D_MODEL = 1024
BATCH = 16
SEQ = 4096
DEPTH = 4

N_MIXERS = 2
EXPAND = 2
D_INNER = EXPAND * D_MODEL
RET_HEADS = 4
RET_QK_DIM = D_MODEL // RET_HEADS
RET_V_DIM = D_INNER // RET_HEADS
RET_CHUNK = 128
SB_HEADS = 16
SB_QK_DIM = D_MODEL // SB_HEADS
SB_V_DIM = D_INNER // SB_HEADS
SB_BLOCK = 128
D_PROJ = 2 * D_MODEL + 2 * D_INNER
N_RET_LAYERS = (DEPTH + 1) // 2
N_SB_LAYERS = DEPTH // 2
RMS_EPS = 1e-6
GN_EPS = 1e-5
ROPE_BASE = 10000.0

kernel_name = "hybrid_retention_stickbreaking_trunk"


def rms_norm(x, g):
    xf = x.astype(jnp.float32)
    y = xf * lax.rsqrt(jnp.mean(xf * xf, axis=-1, keepdims=True) + RMS_EPS)
    return (y * g.astype(jnp.float32)).astype(x.dtype)


def rotate(x, cos, sin):
    x1, x2 = jnp.split(x, 2, axis=-1)
    c = cos[None, :, None, :]
    s = sin[None, :, None, :]
    return jnp.concatenate([x1 * c - x2 * s, x2 * c + x1 * s], axis=-1)


def retention_branch(q, k, v, gn_g):
    B, S, _ = q.shape
    H, dk, dv, C = RET_HEADS, RET_QK_DIM, RET_V_DIM, RET_CHUNK
    n_chunks = S // C
    q = q.astype(jnp.float32).reshape(B, S, H, dk)
    k = k.astype(jnp.float32).reshape(B, S, H, dk)
    v = v.astype(jnp.float32).reshape(B, S, H, dv)
    omega = 1.0 / (ROPE_BASE ** jnp.linspace(0.0, 1.0, dk // 2, dtype=jnp.float32))
    ang = jnp.arange(S, dtype=jnp.float32)[:, None] * omega[None, :]
    cos, sin = jnp.cos(ang), jnp.sin(ang)
    q = rotate(q, cos, sin)
    k = rotate(k, cos, sin) * (dk ** -0.5)
    log_gamma = jnp.log1p(-jnp.exp2(jnp.linspace(-5.0, -9.0, H, dtype=jnp.float32)))
    idx = jnp.arange(C, dtype=jnp.float32)
    diff = idx[:, None] - idx[None, :]
    decay_mat = jnp.where(diff[None] >= 0,
                          jnp.exp(jnp.maximum(diff, 0.0)[None] * log_gamma[:, None, None]), 0.0)
    q_dec = jnp.exp((idx[None, :] + 1.0) * log_gamma[:, None])
    k_dec = jnp.exp((C - 1.0 - idx[None, :]) * log_gamma[:, None])
    chunk_dec = jnp.exp(C * log_gamma)

    def to_chunks(t, d):
        return t.reshape(B, n_chunks, C, H, d).transpose(1, 0, 3, 2, 4)

    qc, kc, vc = to_chunks(q, dk), to_chunks(k, dk), to_chunks(v, dv)

    def body(state, xs):
        qb, kb, vb = xs
        scores = jnp.einsum('bhid,bhjd->bhij', qb, kb) * decay_mat[None]
        inner = jnp.einsum('bhij,bhjv->bhiv', scores, vb)
        cross = jnp.einsum('bhid,bhdv->bhiv', qb * q_dec[None, :, :, None], state)
        new_state = state * chunk_dec[None, :, None, None] + jnp.einsum(
            'bhjd,bhjv->bhdv', kb * k_dec[None, :, :, None], vb)
        return new_state, inner + cross

    state0 = jnp.zeros((B, H, dk, dv), jnp.float32)
    _, out = lax.scan(body, state0, (qc, kc, vc))
    out = out.transpose(1, 0, 3, 2, 4).reshape(B, S, H, dv)
    mu = jnp.mean(out, axis=-1, keepdims=True)
    var = jnp.mean(jnp.square(out - mu), axis=-1, keepdims=True)
    out = (out - mu) * lax.rsqrt(var + GN_EPS)
    return out.reshape(B, S, D_INNER) * gn_g.astype(jnp.float32)


def stick_breaking_branch(q, k, v):
    B, S, _ = q.shape
    H, d, dv = SB_HEADS, SB_QK_DIM, SB_V_DIM
    q = q.astype(jnp.float32).reshape(B, S, H, d).transpose(0, 2, 1, 3)
    k = k.astype(jnp.float32).reshape(B, S, H, d).transpose(0, 2, 1, 3)
    v = v.astype(jnp.float32).reshape(B, S, H, dv).transpose(0, 2, 1, 3)
    scale = d ** -0.5
    outs = []
    for blk in range(S // SB_BLOCK):
        q0 = blk * SB_BLOCK
        kl = q0 + SB_BLOCK
        qb = q[:, :, q0:kl]
        kb = k[:, :, :kl]
        vb = v[:, :, :kl]
        z = jnp.einsum('bhtd,bhsd->bhts', qb, kb) * scale
        t_idx = q0 + jnp.arange(SB_BLOCK)[:, None]
        s_idx = jnp.arange(kl)[None, :]
        mask = s_idx < t_idx
        log_keep = jnp.where(mask, jax.nn.log_sigmoid(-z), 0.0)
        log_keep_next = jnp.pad(log_keep[..., 1:], ((0, 0), (0, 0), (0, 0), (0, 1)))
        log_remain = lax.cumsum(log_keep_next, axis=3, reverse=True)
        a = jnp.where(mask, jnp.exp(jax.nn.log_sigmoid(z) + log_remain), 0.0)
        outs.append(jnp.einsum('bhts,bhsv->bhtv', a, vb))
    o = jnp.concatenate(outs, axis=2)
    return o.transpose(0, 2, 1, 3).reshape(B, S, H * dv)


def setup_inputs(seed: int = 0) -> dict:
    key = jax.random.key(seed)
    ks = jax.random.split(key, 10)
    f32 = jnp.float32
    x = jax.random.normal(ks[0], (BATCH, SEQ, D_MODEL), f32)
    c = jax.random.normal(ks[1], (BATCH, D_MODEL), f32)
    w_in = jax.random.normal(ks[2], (DEPTH, D_MODEL, D_PROJ), f32) * D_MODEL ** -0.5
    w_out = jax.random.normal(ks[3], (DEPTH, D_INNER, D_MODEL), f32) * D_INNER ** -0.5
    w_mod = jax.random.normal(ks[4], (DEPTH, D_MODEL, 3 * D_MODEL), f32) * D_MODEL ** -0.5
    b_mod = jax.random.normal(ks[5], (DEPTH, 3 * D_MODEL), f32) * 0.02
    pre_norm = 1.0 + 0.02 * jax.random.normal(ks[6], (DEPTH, D_MODEL), f32)
    post_norm = 1.0 + 0.02 * jax.random.normal(ks[7], (DEPTH, D_MODEL), f32)
    ret_gn = 1.0 + 0.02 * jax.random.normal(ks[8], (N_RET_LAYERS, D_INNER), f32)
    return {"x": x, "c": c, "w_in": w_in, "w_out": w_out, "w_mod": w_mod,
            "b_mod": b_mod, "pre_norm": pre_norm, "post_norm": post_norm,
            "ret_gn": ret_gn}


def reference(x, c, w_in, w_out, w_mod, b_mod, pre_norm, post_norm, ret_gn):
    c_act = jax.nn.silu(c)
    for i in range(DEPTH):
        mod = c_act @ w_mod[i] + b_mod[i]
        shift, scale, gate = jnp.split(mod, 3, axis=-1)
        h = rms_norm(x, pre_norm[i]) * (1.0 + scale[:, None, :]) + shift[:, None, :]
        proj = h @ w_in[i]
        q, k, v, g = jnp.split(proj, [D_MODEL, 2 * D_MODEL, 2 * D_MODEL + D_INNER], axis=-1)
        if i % N_MIXERS == 0:
            o = retention_branch(q, k, v, ret_gn[i // N_MIXERS])
        else:
            o = stick_breaking_branch(q, k, v)
        y = (o.astype(x.dtype) * jax.nn.silu(g)) @ w_out[i]
        x = x + gate[:, None, :] * rms_norm(y, post_norm[i])
    return x
```

```python
from contextlib import ExitStack

import numpy as np
import ml_dtypes
import concourse.bass as bass
import concourse.mybir as mybir
from concourse.bass_utils import run_bass_kernel_spmd

F32 = mybir.dt.float32
BF16 = mybir.dt.bfloat16
AF = mybir.ActivationFunctionType
ALU = mybir.AluOpType

D = 1024
DP = 6144
DI = 2048
DEPTH = 4
SEQ = 4096
BATCH = 16
NCORES = 8
RMS_EPS = 1e-6
GN_EPS = 1e-5
ENGS = ("pe", "act", "dve", "pool", "sp")


class Op:
    __slots__ = ("eng", "fn", "raw", "oth", "inc", "dma", "tag", "ninc", "sem", "val")

    def __init__(self, eng, fn):
        self.eng = eng
        self.fn = fn
        self.raw = set()
        self.oth = set()
        self.inc = False
        self.dma = False
        self.tag = None
        self.ninc = 1
        self.sem = None
        self.val = None


class Prog:
    def __init__(self, nc):
        self.nc = nc
        self.ops = []
        self.last_w = {}
        self.readers = {}
        self.sem_h = {}
        self.sem_cnt = {}
        self.last_eng = {}
        self.last_tag = {}
        self.barrier = {e: set() for e in ENGS}
        self.know = {e: {} for e in ENGS}
        self.tok_vc = {}
        self.nops = 0
        self.phase = 0

    def _sem(self, name):
        if name not in self.sem_h:
            self.sem_h[name] = self.nc.alloc_semaphore(name=name)
            self.sem_cnt[name] = 0
        return self.sem_h[name]

    def add(self, eng, fn, reads=(), writes=(), dma=False, tag=None, ninc=1):
        op = Op(eng, fn)
        op.dma = dma
        op.ninc = ninc
        if dma:
            op.tag = tag
            op.sem = "d_" + tag
        else:
            op.sem = "c_%s_%d" % (eng, self.phase % 3)
        for r in reads:
            w = self.last_w.get(r)
            if w is not None:
                op.raw.add(w)
        for k in writes:
            w = self.last_w.get(k)
            if w is not None:
                op.oth.add(w)
            for rd in self.readers.get(k, ()):
                op.oth.add(rd)
        if self.barrier[eng]:
            op.raw |= self.barrier[eng]
            self.barrier[eng] = set()
        for r in reads:
            self.readers.setdefault(r, []).append(op)
        for k in writes:
            self.last_w[k] = op
            self.readers[k] = []
        op.raw.discard(op)
        op.oth.discard(op)
        for d in op.raw:
            d.inc = True
        for d in op.oth:
            d.inc = True
        self.ops.append(op)
        if dma:
            self.last_tag[tag] = op
        else:
            self.last_eng[eng] = op
        return op

    def full_barrier(self):
        fr = set(self.last_eng.values()) | set(self.last_tag.values())
        for d in fr:
            d.inc = True
        for e in ENGS:
            self.barrier[e] = set(fr)
        self.last_w = {}
        self.readers = {}

    def flush(self, final=False):
        self.full_barrier()
        nc = self.nc
        for op in self.ops:
            if op.dma:
                op.inc = True
            if op.inc and op.val is None:
                self._sem(op.sem)
                self.sem_cnt[op.sem] += (16 * op.ninc) if op.dma else 1
                op.val = self.sem_cnt[op.sem]
        plan = {}
        for op in self.ops:
            e = op.eng
            K = self.know[e]
            need = {}
            for d in op.raw:
                if (not d.dma) and d.eng == e and e == "pe":
                    continue
                need[d.sem] = max(need.get(d.sem, 0), d.val)
            for d in op.oth:
                if (not d.dma) and d.eng == e and e == "pe":
                    continue
                need[d.sem] = max(need.get(d.sem, 0), d.val)
            own = "c_%s_" % e
            order = sorted(need.items(), key=lambda kv: kv[0].startswith(own))
            waits = []
            for s_, v in order:
                if K.get(s_, 0) >= v:
                    continue
                waits.append((s_, v))
                K[s_] = v
                vc = self.tok_vc.get((s_, v))
                if vc:
                    for s2, v2 in vc.items():
                        if K.get(s2, 0) < v2:
                            K[s2] = v2
            plan[op] = waits
            if op.inc:
                self.tok_vc[(op.sem, op.val)] = {k: v for k, v in K.items() if k.startswith("c_")}
        final_waits = []
        if final:
            K = self.know["sp"]
            fw = {}
            for e in ENGS:
                for d in self.barrier[e]:
                    fw[d.sem] = max(fw.get(d.sem, 0), d.val)
            final_waits = [(s_, v) for s_, v in fw.items() if K.get(s_, 0) < v]
        per = {e: [] for e in ENGS}
        for op in self.ops:
            per[op.eng].append(op)

        def emit(ename, eng):
            for op in per[ename]:
                for s_, v in plan[op]:
                    eng.wait_ge(self.sem_h[s_], v)
                ins = op.fn(eng)
                if op.inc:
                    if op.dma:
                        if not isinstance(ins, (list, tuple)):
                            ins = [ins]
                        assert len(ins) == op.ninc
                        for i_ in ins:
                            i_.then_inc(self.sem_h[op.sem], 16)
                    else:
                        ins.then_inc(self.sem_h[op.sem], 1)
            if final and ename == "sp":
                for s_, v in final_waits:
                    eng.wait_ge(self.sem_h[s_], v)

        with nc.Block() as block:
            @block.tensor
            def _(e):
                emit("pe", e)

            @block.scalar
            def _(e):
                emit("act", e)

            @block.vector
            def _(e):
                emit("dve", e)

            @block.gpsimd
            def _(e):
                emit("pool", e)

            @block.sync
            def _(e):
                emit("sp", e)
        self.nops += len(self.ops)
        self.nwaits = getattr(self, "nwaits", 0) + sum(len(w) for w in plan.values())
        self.ops = []
        self.phase += 1


def make_consts(S):
    bf = ml_dtypes.bfloat16
    i = np.arange(128)
    c = {}
    c["ident"] = np.eye(128, dtype=np.float32).astype(bf)
    c["lneg"] = (-(i[:, None] >= i[None, :]).astype(np.float32)).astype(bf)
    c["oneg"] = (-np.ones((128, 128), np.float32)).astype(bf)
    c["negmask"] = (np.where(i[:, None] >= i[None, :], -30000.0, 0.0).astype(np.float32)).astype(bf)
    c["caus01"] = ((i[None, :] >= i[:, None]).astype(np.float32)).astype(bf)
    dk = 256
    omega = (1.0 / (np.float32(10000.0) ** np.linspace(0.0, 1.0, dk // 2, dtype=np.float32))).astype(np.float32)
    ang = (np.arange(S, dtype=np.float32)[:, None] * omega[None, :]).astype(np.float32)
    c["cosT"] = np.ascontiguousarray(np.cos(ang.astype(np.float64)).T).astype(np.float32)
    c["sinT"] = np.ascontiguousarray(np.sin(ang.astype(np.float64)).T).astype(np.float32)
    lg = np.log1p(-np.exp2(np.linspace(-5.0, -9.0, 4)))
    idx = np.arange(128, dtype=np.float64)
    qf = np.exp((idx[None, :] + 1.0) * lg[:, None])
    kf = np.exp(-(idx[None, :] + 1.0) * lg[:, None]) / 16.0
    c["qfac"] = qf.reshape(1, 512).astype(np.float32)
    c["kfac"] = kf.reshape(1, 512).astype(np.float32)
    cdec = [float(np.exp(128.0 * lg[h])) for h in range(4)]
    return c, cdec


def build_program(S, NSEQ, layers):
    NTOK = S * NSEQ
    NT = NTOK // 128
    NG = NTOK // 512
    TPS = S // 128
    nc = bass.Bass("TRN2", target_bir_lowering=False)
    _, cdec = make_consts(128)

    def din(name, shape, dt=F32):
        return nc.dram_tensor(name, shape, dt, kind="ExternalInput").ap()

    x_in = din("x", [NTOK, D])
    cT_in = din("cT", [128, NSEQ * 8])
    w_in = din("w_in", [DEPTH, D, DP])
    w_out = din("w_out", [DEPTH, DI, D])
    w_mod = din("w_mod", [DEPTH, D, 3 * D])
    b_mod = din("b_mod", [DEPTH, 3 * D])
    pre_norm = din("pre_norm", [DEPTH, D])
    post_norm = din("post_norm", [DEPTH, D])
    gncol_d = din("gn_col", [2, 128, 16])
    ident_d = din("ident", [128, 128], BF16)
    lneg_d = din("lneg", [128, 128], BF16)
    oneg_d = din("oneg", [128, 128], BF16)
    negmask_d = din("negmask", [128, 128], BF16)
    caus_d = din("caus01", [128, 128], BF16)
    cos_d = din("cosT", [128, S])
    sin_d = din("sinT", [128, S])
    qfac_d = din("qfac", [1, 512])
    kfac_d = din("kfac", [1, 512])
    out = nc.dram_tensor("out", [NTOK, D], F32, kind="ExternalOutput").ap()
    qT = nc.dram_tensor("s_qT", [D, NTOK], BF16).ap()
    kT = nc.dram_tensor("s_kT", [D, NTOK], BF16).ap()
    gT = nc.dram_tensor("s_gT", [DI, NTOK], BF16).ap()
    vS = nc.dram_tensor("s_v", [NTOK, DI], BF16).ap()
    ogT = nc.dram_tensor("s_ogT", [DI, NTOK], BF16).ap()

    P = Prog(nc)
    top = ExitStack()

    uid = [0]

    def sb(name, shape, dt, stack=None):
        uid[0] += 1
        return (stack or top).enter_context(nc.sbuf_tensor("sb%d_%s" % (uid[0], name), shape, dt))

    def pt(name, shape, dt, stack):
        uid[0] += 1
        return stack.enter_context(nc.psum_tensor("ps%d_%s" % (uid[0], name), shape, dt))

    ident = sb("ident", [128, 128], BF16)
    lneg = sb("lneg", [128, 128], BF16)
    oneg = sb("oneg", [128, 128], BF16)
    negmask = sb("negmask", [128, 128], BF16)
    caus = sb("caus", [128, 128], BF16)
    qfac = sb("qfac", [128, 512], F32)
    kfac = sb("kfac", [128, 512], F32)
    cb = [sb(f"cb{b}", [128, 8, 128], BF16) for b in range(NSEQ)]
    AM = [sb(f"AM{b}", [128, D], F32) for b in range(NSEQ)]
    SH = [sb(f"SH{b}", [128, D], F32) for b in range(NSEQ)]
    GM = [sb(f"GM{b}", [128, D], F32) for b in range(NSEQ)]

    def dma(eng, out_ap, in_ap, reads, writes, tag):
        P.add(eng, lambda g: g.dma_start(out=out_ap, in_=in_ap), reads=reads, writes=writes, dma=True, tag=tag)

    def mmgroup(items, reads, writes):
        def fn(g):
            ins = None
            for (o, l, r, st, sp_, sk) in items:
                ins = g.matmul(o, lhsT=l, rhs=r, start=st, stop=sp_, skip_group_check=sk)
            return ins
        P.add("pe", fn, reads=reads, writes=writes)

    def trgroup(items, reads, writes):
        def fn(g):
            ins = None
            for (o, i_) in items:
                ins = g.transpose(out=o, in_=i_, identity=ident[:])
            return ins
        P.add("pe", fn, reads=list(reads) + ["ident"], writes=writes)

    def act(out_ap, in_ap, func, reads, writes, **kw):
        P.add("act", lambda g: g.activation(out=out_ap, in_=in_ap, func=func, **kw), reads=reads, writes=writes)

    def tt(eng, out_ap, a, b_, op, reads, writes):
        P.add(eng, lambda g: g.tensor_tensor(out=out_ap, in0=a, in1=b_, op=op), reads=reads, writes=writes)

    def ts(eng, out_ap, a, s1, s2, op0, op1, reads, writes):
        if op1 is None:
            P.add(eng, lambda g: g.tensor_scalar(out=out_ap, in0=a, scalar1=s1, scalar2=None, op0=op0),
                  reads=reads, writes=writes)
        else:
            P.add(eng, lambda g: g.tensor_scalar(out=out_ap, in0=a, scalar1=s1, scalar2=s2, op0=op0, op1=op1),
                  reads=reads, writes=writes)

    def stt(eng, out_ap, a, s, b_, op0, op1, reads, writes):
        P.add(eng, lambda g: g.scalar_tensor_tensor(out=out_ap, in0=a, scalar=s, in1=b_, op0=op0, op1=op1),
              reads=reads, writes=writes)

    def rstd_chain(ssq, rs, key_in, key_out, scale, eps):
        ts("dve", rs, ssq, scale, eps, ALU.mult, ALU.add, [key_in], [key_out])
        act(rs, rs, AF.Ln, [key_out], [key_out])
        act(rs, rs, AF.Exp, [key_out], [key_out], scale=-0.5)

    with ExitStack() as es:
        ct = sb("ct", [128, NSEQ * 8], F32, es)
        cs = sb("cs", [128, NSEQ * 8], F32, es)
        ones = sb("ones", [128, 128], F32, es)
        for nm, t_, d_ in (("ident", ident, ident_d), ("lneg", lneg, lneg_d), ("oneg", oneg, oneg_d),
                           ("negmask", negmask, negmask_d), ("caus", caus, caus_d)):
            dma("sp", t_[:], d_, [], [nm], "c_" + nm)
        dma("sp", qfac[:], qfac_d.partition_broadcast(128), [], ["qfac"], "c_qfac")
        dma("sp", kfac[:], kfac_d.partition_broadcast(128), [], ["kfac"], "c_kfac")
        dma("sp", ct[:], cT_in, [], ["ct"], "c_ct")
        P.add("pool", lambda g: g.memset(ones[:], 1.0), writes=["ones"])
        act(cs[:], ct[:], AF.Silu, ["ct"], ["cs"])
        for b in range(NSEQ):
            for kc in range(8):
                ts("dve", cb[b][:, kc, :], ones[:], cs[:, b * 8 + kc:b * 8 + kc + 1], None, ALU.mult, None,
                   ["ones", "cs"], [("cb", b)])
        P.flush()

    def phase_M(l, win):
        with ExitStack() as es:
            wm = sb("wm", [128, 8, 3 * D], BF16, es)
            bmod = sb("bmod", [128, 3 * D], F32, es)
            pre = sb("pre", [128, D], F32, es)
            post = sb("post", [128, D], F32, es)
            ps = pt("psM", [128, 6, 512], F32, es)
            dma("sp", bmod[:], b_mod[l:l + 1, :].partition_broadcast(128), [], ["bmod"], "bmod")
            dma("sp", pre[:], pre_norm[l:l + 1, :].partition_broadcast(128), [], ["pre"], "pre")
            dma("sp", post[:], post_norm[l:l + 1, :].partition_broadcast(128), [], ["post"], "post")
            for kc in range(8):
                dma("pool", wm[:, kc, :], w_mod[l, kc * 128:(kc + 1) * 128, :], [], [("wm", kc)], f"wm{kc % 2}")
            for kc in range(8):
                dma("pool", win[:, kc, :], w_in[l, kc * 128:(kc + 1) * 128, :], [], [("win", kc)], f"win{kc}")
            for b in range(NSEQ):
                items = [(ps[:, n, :], cb[b][:, kc, :], wm[:, kc, n * 512:(n + 1) * 512], kc == 0, kc == 7, False)
                         for n in range(6) for kc in range(8)]
                mmgroup(items, [("wm", kc) for kc in range(8)] + [("cb", b)], ["psM"])
                psv = ps[:, :, :].rearrange("p a b -> p (a b)")
                tt("dve", SH[b][:], psv[:, 0:D], bmod[:, 0:D], ALU.add, ["psM", "bmod"], [("SH", b)])
                tt("dve", AM[b][:], psv[:, D:2 * D], bmod[:, D:2 * D], ALU.add, ["psM", "bmod"], [("AM", b)])
                tt("dve", GM[b][:], psv[:, 2 * D:3 * D], bmod[:, 2 * D:3 * D], ALU.add, ["psM", "bmod"], [("GM", b)])
                stt("dve", AM[b][:], AM[b][:], 1.0, pre[:], ALU.add, ALU.mult, [("AM", b), "pre"], [("AM", b)])
                tt("pool", GM[b][:], GM[b][:], post[:], ALU.mult, [("GM", b), "post"], [("GM", b)])
            P.flush()

    def phase_A(l, xsrc, qscale, win):
        with ExitStack() as es:
            xt = [sb(f"xtA{i}", [128, D], F32, es) for i in range(4)]
            junk = sb("junkA", [128, D], BF16, es)
            ssq = [sb(f"ssqA{i}", [128, 1], F32, es) for i in range(4)]
            rs = [sb(f"rsA{i}", [128, 1], F32, es) for i in range(4)]
            t1 = [sb(f"t1A{i}", [128, D], F32, es) for i in range(2)]
            hb = [sb(f"hbA{i}", [128, D], BF16, es) for i in range(4)]
            hT = [sb(f"hTA{i}", [128, 8, 512], BF16, es) for i in range(2)]
            stf = [sb(f"stfA{i}", [128, 512], BF16, es) for i in range(4)]
            stv = [sb(f"stvA{i}", [128, DI], BF16, es) for i in range(2)]
            pst = pt("pstA", [128, 2, 1024], BF16, es)
            ps = pt("psA", [128, 6, 512], F32, es)
            winkeys = [("win", kc) for kc in range(8)]
            cnt = dict(bank=0, nst=0, nsv=0)

            def prepA(g):
                Ts = [g * 4 + t4 for t4 in range(4)]
                for T in Ts:
                    i4 = T % 4
                    dma("sp", xt[i4][:], xsrc[T * 128:(T + 1) * 128, :], [], [("xt", i4)], f"xtA{i4}")
                for T in Ts:
                    i4 = T % 4
                    act(junk[:], xt[i4][:], AF.Square, [("xt", i4)], ["junk", ("ssq", i4)], accum_out=ssq[i4][:])
                for T in Ts:
                    i4 = T % 4
                    ts("dve", rs[i4][:], ssq[i4][:], 1.0 / D, RMS_EPS, ALU.mult, ALU.add, [("ssq", i4)], [("rs", i4)])
                for T in Ts:
                    i4 = T % 4
                    act(rs[i4][:], rs[i4][:], AF.Ln, [("rs", i4)], [("rs", i4)])
                for T in Ts:
                    i4 = T % 4
                    act(rs[i4][:], rs[i4][:], AF.Exp, [("rs", i4)], [("rs", i4)], scale=-0.5)
                for T in Ts:
                    b = (T * 128) // S
                    i2 = T % 2
                    i4 = T % 4
                    stt("dve", t1[i2][:], xt[i4][:], rs[i4][:, 0:1], AM[b][:], ALU.mult, ALU.mult,
                        [("xt", i4), ("rs", i4), ("AM", b)], [("t1", i2)])
                    tt("pool", hb[i4][:], t1[i2][:], SH[b][:], ALU.add, [("t1", i2), ("SH", b)], [("hb", i4)])

            def prepB(g):
                hk = ("hT", g % 2)
                for t4 in range(4):
                    T = g * 4 + t4
                    i2 = T % 2
                    i4 = T % 4
                    trgroup([(pst[:, i2, kc * 128:(kc + 1) * 128], hb[i4][:, kc * 128:(kc + 1) * 128]) for kc in range(8)],
                            [("hb", i4)], [("pst", i2)])
                    act(hT[g % 2][:, :, t4 * 128:(t4 + 1) * 128], pst[:, i2, :].rearrange("p (a b) -> p a b", a=8),
                        AF.Copy, [("pst", i2)], [("hT", g % 2, t4)])

            def projF(g):
                hk = ("hT", g % 2)
                fm = [(qT, n, n * 128, "q") for n in range(8)] + [(kT, n, D + n * 128, "k") for n in range(8)] + \
                     [(gT, n, 2 * D + DI + n * 128, "g") for n in range(16)]
                for (dst, n, col, kind) in fm:
                    bk = cnt["bank"] % 6
                    cnt["bank"] += 1
                    items = [(ps[:, bk, :], win[:, kc, col:col + 128], hT[g % 2][:, kc, :], kc == 0, kc == 7, False)
                             for kc in range(8)]
                    mmgroup(items, winkeys + [("hT", g % 2, q4) for q4 in range(4)], [("ps", bk)])
                    si = cnt["nst"] % 4
                    cnt["nst"] += 1
                    if kind == "q":
                        ts("dve", stf[si][:], ps[:, bk, :], float(qscale), None, ALU.mult, None, [("ps", bk)], [("stf", si)])
                    elif kind == "k":
                        P.add("dve", lambda g_, si=si, bk=bk: g_.tensor_copy(out=stf[si][:], in_=ps[:, bk, :]),
                              reads=[("ps", bk)], writes=[("stf", si)])
                    else:
                        act(stf[si][:], ps[:, bk, :], AF.Silu, [("ps", bk)], [("stf", si)])
                    dma("sp", dst[n * 128:(n + 1) * 128, g * 512:(g + 1) * 512], stf[si][:], [("stf", si)], [], f"stfA{si}")

            def projV(g):
                hk = ("hT", g % 2)
                for t4 in range(4):
                    T = g * 4 + t4
                    sv = cnt["nsv"] % 2
                    cnt["nsv"] += 1
                    for nv in range(4):
                        bk = cnt["bank"] % 6
                        cnt["bank"] += 1
                        col = 2 * D + nv * 512
                        items = [(ps[:, bk, :], hT[g % 2][:, kc, t4 * 128:(t4 + 1) * 128], win[:, kc, col:col + 512],
                                  kc == 0, kc == 7, False) for kc in range(8)]
                        mmgroup(items, winkeys + [("hT", g % 2, t4)], [("ps", bk)])
                        if nv % 2 == 0:
                            P.add("dve", lambda g_, sv=sv, bk=bk, nv=nv: g_.tensor_copy(
                                out=stv[sv][:, nv * 512:(nv + 1) * 512], in_=ps[:, bk, :]),
                                reads=[("ps", bk)], writes=[("stv", sv)])
                        else:
                            act(stv[sv][:, nv * 512:(nv + 1) * 512], ps[:, bk, :], AF.Copy, [("ps", bk)], [("stv", sv)])
                    dma("sp", vS[T * 128:(T + 1) * 128, :], stv[sv][:], [("stv", sv)], [], f"stvA{sv}")

            prepA(0)
            prepB(0)
            for g in range(NG):
                if g + 1 < NG:
                    prepA(g + 1)
                projF(g)
                if g + 1 < NG:
                    prepB(g + 1)
                projV(g)
            P.flush()

    def phase_B_sb(l, wo_pre=None):
        QW = 1024
        NQC = S // QW
        KPC = QW // 128
        with ExitStack() as es:
            qh = [sb(f"qh{i}", [128, S], BF16, es) for i in range(2)]
            kh = [sb(f"kh{i}", [128, S], BF16, es) for i in range(2)]
            vh = [sb(f"vh{i}", [128, TPS, 128], BF16, es) for i in range(2)]
            sgh = [sb(f"sgh{i}", [128, S], BF16, es) for i in range(2)]
            e_sb = [sb(f"e_sb{i}", [128, QW], F32, es) for i in range(4)]
            sp_sb = [sb(f"sp_sb{i}", [128, QW], BF16, es) for i in range(3)]
            p_sb = [sb(f"p_sb{i}", [128, QW], F32, es) for i in range(2)]
            a_sb = [sb(f"a_sb{i}", [128, QW], BF16, es) for i in range(3)]
            ssb = [sb(f"ssb{i}", [128, QW], BF16, es) for i in range(2)]
            ogs = [sb(f"ogsB{i}", [128, QW], BF16, es) for i in range(2)]
            pz = pt("pzB", [128, 4, 512], F32, es)
            pw = pt("pwB", [128, 2, 512], F32, es)
            po = pt("poB", [128, 2, 512], F32, es)
            pwf = pw[:, :, :].rearrange("p a b -> p (a b)")
            pof = po[:, :, :].rearrange("p a b -> p (a b)")
            if wo_pre is not None:
                for kc in range(16):
                    dma("pool", wo_pre[:, kc, :], w_out[l, kc * 128:(kc + 1) * 128, :], [], [("wo", kc)], f"wo{kc % 4}")
            for i in range(2):
                P.add("dve", lambda g, i=i: g.memset(qh[i][64:128, :], 0.0), writes=[("qh", i)])
                P.add("pool", lambda g, i=i: g.memset(kh[i][64:128, :], 0.0), writes=[("kh", i)])
            tiles = []
            hc = 0
            cc = 0
            for b in range(NSEQ):
                for h in range(16):
                    for qc in range(NQC):
                        tp = KPC * qc + KPC - 1
                        for kb in range(tp, -1, -1):
                            tiles.append(dict(b=b, h=h, qc=qc, kb=kb, j=tp - kb, top=tp, hc=hc, cc=cc,
                                              first=(qc == 0 and kb == tp)))
                        cc += 1
                    hc += 1
            n = len(tiles)

            def geom(t):
                kb, qc = t["kb"], t["qc"]
                diag = kb >= KPC * qc
                c0 = 128 * (kb - KPC * qc) if diag else 0
                segs = []
                if c0 < 512:
                    segs.append((0, c0, 512))
                segs.append((1, max(c0, 512), QW))
                return diag, c0, segs

            def load_head(hc_):
                if hc_ >= NSEQ * 16:
                    return
                b_, h_ = hc_ // 16, hc_ % 16
                p2 = hc_ % 2
                tok = slice(b_ * S, (b_ + 1) * S)
                dma("sp", qh[p2][0:64, :], qT[h_ * 64:(h_ + 1) * 64, tok], [], [("qh", p2)], f"qh{p2}")
                dma("sp", kh[p2][0:64, :], kT[h_ * 64:(h_ + 1) * 64, tok], [], [("kh", p2)], f"kh{p2}")
                dma("sp", vh[p2][:], vS[tok, h_ * 128:(h_ + 1) * 128].rearrange("(kb p) d -> p kb d", p=128),
                    [], [("vh", p2)], f"vh{p2}")
                dma("sp", sgh[p2][:], gT[h_ * 128:(h_ + 1) * 128, tok], [], [("sgh", p2)], f"sgh{p2}")

            def prefetch(i):
                t = tiles[i]
                if t["qc"] == 0 and t["kb"] == 0:
                    load_head(t["hc"] + 1)

            def st1(i):
                t = tiles[i]
                qc, kb = t["qc"], t["kb"]
                s2 = t["hc"] % 2
                if t["first"] and t["hc"] == 0:
                    load_head(0)
                diag, c0, segs = geom(t)
                t0 = qc * QW
                pp = i % 2
                items = []
                rd = [("qh", s2), ("kh", s2)]
                for (bk, lo, hi) in segs:
                    has_mask = diag and lo <= c0 < hi
                    items.append((pz[:, 2 * pp + bk, lo - 512 * bk:hi - 512 * bk], kh[s2][:, kb * 128:(kb + 1) * 128],
                                  qh[s2][:, t0 + lo:t0 + hi], True, not has_mask, False))
                    if has_mask:
                        items.append((pz[:, 2 * pp + bk, c0 - 512 * bk:c0 - 512 * bk + 128], ident[:], negmask[:],
                                      False, True, True))
                if diag:
                    rd += ["ident", "negmask"]
                mmgroup(items, rd, [("pz", pp)])
                pzf = pz[:, 2 * pp:2 * pp + 2, :].rearrange("p a b -> p (a b)")
                act(e_sb[i % 4][:, c0:QW], pzf[:, c0:QW], AF.Exp, [("pz", pp)], [("e", i % 4)])

            def st1b(i):
                t = tiles[i]
                diag, c0, segs = geom(t)
                act(sp_sb[i % 3][:, c0:QW], e_sb[i % 4][:, c0:QW], AF.Ln, [("e", i % 4)], [("sp", i % 3)], bias=1.0)

            def st2(i):
                t = tiles[i]
                kb, j = t["kb"], t["j"]
                diag, c0, segs = geom(t)
                y = i % 3
                w2 = i % 2
                items = []
                rd = [("sp", y), "lneg"]
                for (bk, lo, hi) in segs:
                    items.append((pw[:, bk, lo - 512 * bk:hi - 512 * bk], lneg[:], sp_sb[y][:, lo:hi], True, j == 0, False))
                    if j > 0:
                        items.append((pw[:, bk, lo - 512 * bk:hi - 512 * bk], oneg[:], ssb[j % 2][:, lo:hi], False, True, False))
                if j > 0:
                    rd += ["oneg", ("ss", j % 2), ("ssz", j % 2)]
                mmgroup(items, rd, ["pw"])
                if kb > 0:
                    nx = (j + 1) % 2
                    e = "pool" if i % 2 == 0 else "dve"
                    if c0 > 0:
                        P.add(e, lambda g: g.memset(ssb[nx][:, 0:c0], 0.0), writes=[("ssz", nx)])
                    if j == 0:
                        P.add(e, lambda g: g.tensor_copy(out=ssb[nx][:, c0:QW], in_=sp_sb[y][:, c0:QW]),
                              reads=[("sp", y)], writes=[("ss", nx)])
                    else:
                        tt(e, ssb[nx][:, c0:QW], ssb[j % 2][:, c0:QW], sp_sb[y][:, c0:QW], ALU.add,
                           [("ss", j % 2), ("sp", y)], [("ss", nx)])
                act(p_sb[w2][:, c0:QW], pwf[:, c0:QW], AF.Exp, ["pw"], [("p", w2)])

            def st3(i):
                t = tiles[i]
                diag, c0, segs = geom(t)
                tt("dve", a_sb[i % 3][:, c0:QW], e_sb[i % 4][:, c0:QW], p_sb[i % 2][:, c0:QW], ALU.mult,
                   [("e", i % 4), ("p", i % 2)], [("a", i % 3)])

            def st4(i):
                t = tiles[i]
                b, h, qc, kb, j = t["b"], t["h"], t["qc"], t["kb"], t["j"]
                s2 = t["hc"] % 2
                diag, c0, segs = geom(t)
                items = []
                for (bk, lo, hi) in segs:
                    first = (j == 0) if bk == 1 else (j == KPC // 2)
                    items.append((po[:, bk, lo - 512 * bk:hi - 512 * bk], vh[s2][:, kb, :], a_sb[i % 3][:, lo:hi],
                                  first, kb == 0, True))
                mmgroup(items, [("vh", s2), ("a", i % 3)], ["po"])
                if kb == 0:
                    t0 = qc * QW
                    o2 = t["cc"] % 2
                    tt("dve", ogs[o2][:], pof, sgh[s2][:, t0:t0 + QW], ALU.mult, ["po", ("sgh", s2)], [("ogs", o2)])
                    dma("sp", ogT[h * 128:(h + 1) * 128, b * S + t0:b * S + t0 + QW], ogs[o2][:],
                        [("ogs", o2)], [], f"ogsB{o2}")

            for step in range(n + 4):
                if step < n:
                    st1(step)
                if 0 <= step - 1 < n:
                    st1b(step - 1)
                if 0 <= step - 2 < n:
                    st2(step - 2)
                if 0 <= step - 3 < n:
                    st3(step - 3)
                if 0 <= step - 4 < n:
                    st4(step - 4)
                if step < n:
                    prefetch(step)
            P.flush()

    def phase_B_ret(l):
        gi = l // 2
        GT = 256
        CH = GT // 128
        NGR = S // GT
        with ExitStack() as es:
            qg = [sb(f"qgR{i}", [128, 8, GT], BF16, es) for i in range(2)]
            kg = [sb(f"kgR{i}", [128, 8, GT], BF16, es) for i in range(2)]
            vg = [sb(f"vgR{i}", [128, CH, DI], BF16, es) for i in range(2)]
            sgg = [sb(f"sggR{i}", [128, 16, GT], BF16, es) for i in range(2)]
            cs_ = [sb(f"cosR{i}", [128, GT], F32, es) for i in range(2)]
            sn_ = [sb(f"sinR{i}", [128, GT], F32, es) for i in range(2)]
            ta = {(e, i): sb(f"taR{e}{i}", [128, GT], F32, es) for e in ("dve", "pool") for i in range(2)}
            tb = {(e, i): sb(f"tbR{e}{i}", [128, GT], F32, es) for e in ("dve", "pool") for i in range(2)}
            qr = [sb(f"qrR{i}", [128, 8, GT], BF16, es) for i in range(2)]
            kr = [sb(f"krR{i}", [128, 8, GT], BF16, es) for i in range(2)]
            ktok = sb("ktokR", [128, CH, D], BF16, es)
            st = [sb(f"stR{h}", [128, 2, 512], F32, es) for h in range(4)]
            stb = [sb(f"stbR{h}", [128, 2, 512], BF16, es) for h in range(4)]
            scm = [sb(f"scmR{i}", [128, 4, 128], BF16, es) for i in range(2)]
            gncol = sb("gncolR", [128, 16], F32, es)
            bst = [sb(f"bstR{i}", [128, 6], F32, es) for i in range(4)]
            mv = [sb(f"mvR{i}", [128, 2], F32, es) for i in range(4)]
            rsg = [sb(f"rsgR{i}", [128, 1], F32, es) for i in range(4)]
            ob = [sb(f"obR{i}", [128, DI], BF16, es) for i in range(2)]
            ogs = sb("ogsR", [128, 16, GT], BF16, es)
            psf = pt("psfR", [128, 6, 512], F32, es)
            psb = pt("psbR", [128, 2, 1024], BF16, es)
            dma("sp", gncol[:], gncol_d[gi], [], ["gncol"], "gnb")
            groups = [(b, g4) for b in range(NSEQ) for g4 in range(NGR)]
            NGRP = len(groups)

            def load(gx):
                b, g4 = groups[gx]
                z = gx % 2
                p0 = g4 * GT
                tk = slice(b * S + p0, b * S + p0 + GT)
                dma("sp", qg[z][:], qT[:, tk].rearrange("(fc p) t -> p fc t", p=128), [], [("qg", z)], f"qgR{z}")
                dma("sp", kg[z][:], kT[:, tk].rearrange("(fc p) t -> p fc t", p=128), [], [("kg", z)], f"kgR{z}")
                dma("sp", cs_[z][:], cos_d[:, p0:p0 + GT], [], [("cos", z)], f"cosR{z}")
                dma("sp", sn_[z][:], sin_d[:, p0:p0 + GT], [], [("sin", z)], f"sinR{z}")
                dma("sp", vg[z][:], vS[tk, :].rearrange("(tt p) d -> p tt d", p=128), [], [("vg", z)], f"vgR{z}")
                dma("sp", sgg[z][:], gT[:, tk].rearrange("(fc p) t -> p fc t", p=128), [], [("sgg", z)], f"sggR{z}")

            def rope_units(gx):
                z = gx % 2
                units = []
                k_ = 0
                ecnt = {}
                for h in range(4):
                    for (src, skey, dst, dkey, fac, fkey) in ((qg[z], ("qg", z), qr[z], "qr", qfac, "qfac"),
                                                              (kg[z], ("kg", z), kr[z], "kr", kfac, "kfac")):
                        for half in range(2):
                            en_ = "dve" if k_ % 5 in (0, 2) else "pool"
                            ecnt[en_] = ecnt.get(en_, 0) + 1
                            e = (en_, ecnt[en_] % 2)
                            k_ += 1

                            def unit(e=e, h=h, src=src, skey=skey, dst=dst, dkey=dkey, fac=fac, fkey=fkey, half=half):
                                fbc = fac[:, h * 128:(h + 1) * 128].unsqueeze(1).broadcast_to([128, CH, 128])
                                xa = src[:, 2 * h + half, :]
                                xb_ = src[:, 2 * h + 1 - half, :]
                                tak, tbk = ("ta", e), ("tb", e)
                                en = e[0]
                                tt(en, ta[e][:], xa, cs_[z][:], ALU.mult, [skey, ("cos", z)], [tak])
                                tt(en, tb[e][:], xb_, sn_[z][:], ALU.mult, [skey, ("sin", z)], [tbk])
                                tt(en, ta[e][:], ta[e][:], tb[e][:], ALU.subtract if half == 0 else ALU.add,
                                   [tak, tbk], [tak])
                                tt(en, dst[:, 2 * h + half, :].rearrange("p (a b) -> p a b", a=CH),
                                   ta[e][:].rearrange("p (a b) -> p a b", a=CH), fbc, ALU.mult,
                                   [tak, fkey], [(dkey, z, 2 * h + half)])
                            units.append(unit)
                return units

            cnts = dict(o=0, c=0)

            def gnfold(gx):
                z = gx % 2
                for fc in range(16):
                    tt("pool", sgg[z][:, fc, :], sgg[z][:, fc, :], gncol[:, fc:fc + 1].broadcast_to([128, GT]), ALU.mult,
                       [("sgg", z), "gncol"], [("sggf", z, fc)])

            def chunks(gx, pending):
                b, g4 = groups[gx]
                z = gx % 2
                p0 = g4 * GT
                tk = slice(b * S + p0, b * S + p0 + GT)
                if g4 == 0:
                    for h in range(4):
                        P.add("pool", lambda g, h=h: g.memset(st[h][:], 0.0), writes=[("st", h)])
                        P.add("pool", lambda g, h=h: g.memset(stb[h][:], 0.0), writes=[("stb", h)])
                if gx == 0:
                    gnfold(0)
                if gx + 1 < NGRP:
                    gnfold(gx + 1)
                krkeys = [("kr", z, i) for i in range(8)]
                qrkeys = [("qr", z, i) for i in range(8)]
                for c in range(CH):
                    tsl = slice(c * 128, (c + 1) * 128)
                    trgroup([(psb[:, 0, fc * 128:(fc + 1) * 128], kr[z][:, fc, tsl]) for fc in range(8)],
                            krkeys, [("psb", 0)])
                    for h in range(4):
                        act(ktok[:, c, h * 256:(h + 1) * 256], psb[:, 0, h * 256:(h + 1) * 256], AF.Copy,
                            [("psb", 0)], [("ktok", c, h)], scale=cdec[h])
                for c in range(CH):
                    tsl = slice(c * 128, (c + 1) * 128)
                    items = []
                    for h in range(4):
                        for half in range(2):
                            items.append((psf[:, 0, h * 128:(h + 1) * 128], kr[z][:, 2 * h + half, tsl],
                                          qr[z][:, 2 * h + half, tsl], half == 0, half == 1, True))
                    mmgroup(items, krkeys + qrkeys, [("psf", 0)])
                    cc = cnts["c"]
                    cnts["c"] += 1
                    sc = scm[cc % 2]
                    sck = ("scm", cc % 2)
                    tt("dve", sc[:], psf[:, 0, :].rearrange("p (a b) -> p a b", a=4),
                       caus[:].unsqueeze(1).broadcast_to([128, 4, 128]), ALU.mult, [("psf", 0), "caus"], [sck])
                    obk = ("ob", cc % 2)
                    obt = ob[cc % 2]
                    o2s = {}

                    def o_mm(h):
                        o2 = 1 + cnts["o"] % 3
                        cnts["o"] += 1
                        o2s[h] = o2
                        vsl = vg[z][:, c, h * 512:(h + 1) * 512]
                        mmgroup([(psf[:, o2, :], sc[:, h, :], vsl, True, False, False),
                                 (psf[:, o2, :], qr[z][:, 2 * h, tsl], stb[h][:, 0, :], False, False, False),
                                 (psf[:, o2, :], qr[z][:, 2 * h + 1, tsl], stb[h][:, 1, :], False, True, False)],
                                [sck, ("vg", z), ("qr", z, 2 * h), ("qr", z, 2 * h + 1), ("stb", h)], [("psf", o2)])

                    def state_upd(h):
                        vsl = vg[z][:, c, h * 512:(h + 1) * 512]
                        mmgroup([(psf[:, 4 + half, :], ktok[:, c, h * 256 + half * 128:h * 256 + (half + 1) * 128],
                                  vsl, True, True, False) for half in range(2)],
                                [("ktok", c, h), ("vg", z)], [("psf", 4)])
                        stt("dve", st[h][:].rearrange("p a b -> p (a b)"), st[h][:].rearrange("p a b -> p (a b)"),
                            cdec[h], psf[:, 4:6, :].rearrange("p a b -> p (a b)"), ALU.mult, ALU.add,
                            [("st", h), ("psf", 4)], [("st", h)])
                        act(stb[h][:], st[h][:], AF.Copy, [("st", h)], [("stb", h)])

                    def gn_stats(h):
                        o2 = o2s[h]
                        P.add("dve", lambda g, h=h, o2=o2: g.bn_stats(out=bst[h][:], in_=psf[:, o2, :]),
                              reads=[("psf", o2)], writes=[("bst", h)])
                        P.add("dve", lambda g, h=h: g.bn_aggr(out=mv[h][:], in_=bst[h][:]),
                              reads=[("bst", h)], writes=[("mv", h)])
                        ts("dve", rsg[h][:], mv[h][:, 1:2], 1.0, GN_EPS, ALU.mult, ALU.add, [("mv", h)], [("rsg", h)])
                        act(rsg[h][:], rsg[h][:], AF.Ln, [("rsg", h)], [("rsg", h)])
                        act(rsg[h][:], rsg[h][:], AF.Exp, [("rsg", h)], [("rsg", h)], scale=-0.5)

                    def gn_norm(h):
                        o2 = o2s[h]
                        ts("dve", obt[:, h * 512:(h + 1) * 512], psf[:, o2, :], mv[h][:, 0:1], rsg[h][:, 0:1],
                           ALU.subtract, ALU.mult, [("psf", o2), ("mv", h), ("rsg", h)], [obk])

                    def fill(k):
                        for _ in range(k):
                            if pending:
                                pending.pop(0)()

                    o_mm(0)
                    o_mm(1)
                    o_mm(2)
                    state_upd(0)
                    gn_stats(0)
                    state_upd(1)
                    gn_stats(1)
                    gn_norm(0)
                    o_mm(3)
                    state_upd(2)
                    gn_stats(2)
                    gn_norm(1)
                    state_upd(3)
                    gn_stats(3)
                    gn_norm(2)
                    fill(16 // CH)
                    gn_norm(3)
                    for hf in range(2):
                        trgroup([(psb[:, hf, fc * 128:(fc + 1) * 128], obt[:, (hf * 8 + fc) * 128:(hf * 8 + fc + 1) * 128])
                                 for fc in range(8)], [obk], [("psb", hf)])
                        tt("dve", ogs[:, hf * 8:(hf + 1) * 8, tsl], psb[:, hf, :].rearrange("p (a b) -> p a b", a=8),
                           sgg[z][:, hf * 8:(hf + 1) * 8, tsl], ALU.mult,
                           [("psb", hf), ("sgg", z)] + [("sggf", z, hf * 8 + q) for q in range(8)], ["ogs"])
                while pending:
                    pending.pop(0)()
                dma("sp", ogT[:, tk].rearrange("(fc p) t -> p fc t", p=128), ogs[:], ["ogs"], [], "ogsR")

            load(0)
            if NGRP > 1:
                load(1)
            for u in rope_units(0):
                u()
            for gx in range(NGRP):
                pending = rope_units(gx + 1) if gx + 1 < NGRP else []
                chunks(gx, pending)
                if gx + 2 < NGRP:
                    load(gx + 2)
            P.flush()

    def phase_C(l, xsrc, wo_pre=None):
        with ExitStack() as es:
            wo = wo_pre if wo_pre is not None else sb("woC", [128, 16, D], BF16, es)
            ogg = [sb(f"oggC{i}", [128, 16, 512], BF16, es) for i in range(2)]
            xt = [sb(f"xtC{i}", [128, D], F32, es) for i in range(2)]
            junk = sb("junkC", [128, D], BF16, es)
            ssq = [sb(f"ssqC{i}", [128, 1], F32, es) for i in range(2)]
            rs = [sb(f"rsC{i}", [128, 1], F32, es) for i in range(2)]
            t1 = [sb(f"t1C{i}", [128, D], F32, es) for i in range(2)]
            xo = [sb(f"xoC{i}", [128, D], F32, es) for i in range(2)]
            py = pt("pyC", [128, 4, 512], F32, es)
            if wo_pre is None:
                for kc in range(16):
                    dma("pool", wo[:, kc, :], w_out[l, kc * 128:(kc + 1) * 128, :], [], [("wo", kc)], f"wo{kc % 4}")
            wokeys = [("wo", kc) for kc in range(16)]
            def ld_ogg(g):
                dma("sp", ogg[g % 2][:], ogT[:, g * 512:(g + 1) * 512].rearrange("(kc p) t -> p kc t", p=128),
                    [], [("ogg", g % 2)], f"oggC{g % 2}")

            def ld_xt(T):
                dma("sp", xt[T % 2][:], xsrc[T * 128:(T + 1) * 128, :], [], [("xt", T % 2)], f"xtC{T % 2}")

            ld_ogg(0)
            ld_xt(0)
            for g in range(NG):
                g2 = g % 2
                if g + 1 < NG:
                    ld_ogg(g + 1)
                for t4 in range(4):
                    T = g * 4 + t4
                    b = (T * 128) // S
                    i2 = T % 2
                    if T + 1 < NT:
                        ld_xt(T + 1)
                    items = []
                    for hf in range(2):
                        for kc in range(16):
                            items.append((py[:, 2 * i2 + hf, :], ogg[g2][:, kc, t4 * 128:(t4 + 1) * 128],
                                          wo[:, kc, hf * 512:(hf + 1) * 512], kc == 0, kc == 15, False))
                    mmgroup(items, wokeys + [("ogg", g2)], [("py", i2)])
                    yv = py[:, 2 * i2:2 * i2 + 2, :].rearrange("p a b -> p (a b)")
                    act(junk[:], yv, AF.Square, [("py", i2)], ["junk", ("ssq", i2)], accum_out=ssq[i2][:])
                    rstd_chain(ssq[i2][:], rs[i2][:], ("ssq", i2), ("rs", i2), 1.0 / D, RMS_EPS)
                    stt("dve", t1[i2][:], yv, rs[i2][:, 0:1], GM[b][:], ALU.mult, ALU.mult,
                        [("py", i2), ("rs", i2), ("GM", b)], [("t1", i2)])
                    tt("pool", xo[i2][:], t1[i2][:], xt[i2][:], ALU.add, [("t1", i2), ("xt", i2)], [("xo", i2)])
                    dma("sp", out[T * 128:(T + 1) * 128, :], xo[i2][:], [("xo", i2)], [], f"xoC{i2}")
            P.flush()

    first = True
    for l in layers:
        xsrc = x_in if first else out
        first = False
        with ExitStack() as esw:
            win_l = sb("win", [128, 8, DP], BF16, esw)
            phase_M(l, win_l)
            phase_A(l, xsrc, 1.0 if l % 2 == 0 else 0.125, win_l)
        if l % 2 == 0:
            phase_B_ret(l)
            phase_C(l, xsrc)
        else:
            with ExitStack() as es2:
                wo_pre = sb("woP", [128, 16, D], BF16, es2)
                phase_B_sb(l, wo_pre)
                phase_C(l, xsrc, wo_pre)
    P.add("sp", lambda g: g.nop(), reads=[], writes=[])
    P.flush(final=True)
    top.close()
    return nc, P


_CACHE = {}


def _get_program():
    if "nc" not in _CACHE:
        _CACHE["nc"] = build_program(SEQ, BATCH // NCORES, list(range(DEPTH)))[0]
    return _CACHE["nc"]


def make_in_maps(x, c, w_in, w_out, w_mod, b_mod, pre_norm, post_norm, ret_gn, S, nseq, ncores):
    consts, _ = make_consts(S)
    f = lambda a: np.ascontiguousarray(np.asarray(a, dtype=np.float32))
    shared = {"w_in": f(w_in), "w_out": f(w_out), "w_mod": f(w_mod), "b_mod": f(b_mod),
              "pre_norm": f(pre_norm), "post_norm": f(post_norm),
              "gn_col": np.ascontiguousarray(f(ret_gn).reshape(2, 16, 128).transpose(0, 2, 1))}
    shared.update(consts)
    x = f(x)
    c = f(c)
    maps = []
    for i in range(ncores):
        xs = x[i * nseq:(i + 1) * nseq].reshape(nseq * S, D)
        cc = c[i * nseq:(i + 1) * nseq]
        cT = np.ascontiguousarray(cc.reshape(nseq, 8, 128).transpose(2, 0, 1).reshape(128, nseq * 8))
        m = dict(shared)
        m["x"] = np.ascontiguousarray(xs)
        m["cT"] = cT
        maps.append(m)
    return maps


def kernel(x, c, w_in, w_out, w_mod, b_mod, pre_norm, post_norm, ret_gn):
    nseq = BATCH // NCORES
    nc = _get_program()
    maps = make_in_maps(x, c, w_in, w_out, w_mod, b_mod, pre_norm, post_norm, ret_gn, SEQ, nseq, NCORES)
    res = run_bass_kernel_spmd(nc, maps, core_ids=list(range(NCORES)))
    outs = [np.asarray(r["out"]).reshape(nseq, SEQ, D) for r in res.results]
    return np.concatenate(outs, axis=0).astype(np.float32)
```

```python
from contextlib import ExitStack

import numpy as np
import ml_dtypes
import concourse.bass as bass
import concourse.mybir as mybir
from concourse.bass_utils import run_bass_kernel_spmd

F32 = mybir.dt.float32
BF16 = mybir.dt.bfloat16
AF = mybir.ActivationFunctionType
ALU = mybir.AluOpType

D = 1024
DP = 6144
DI = 2048
DEPTH = 4
SEQ = 4096
BATCH = 16
NCORES = 8
RMS_EPS = 1e-6
GN_EPS = 1e-5
ENGS = ("pe", "act", "dve", "pool", "sp")


class Op:
    __slots__ = ("eng", "fn", "raw", "oth", "inc", "dma", "tag", "ninc", "sem", "val")

    def __init__(self, eng, fn):
        self.eng = eng
        self.fn = fn
        self.raw = set()
        self.oth = set()
        self.inc = False
        self.dma = False
        self.tag = None
        self.ninc = 1
        self.sem = None
        self.val = None


class Prog:
    def __init__(self, nc):
        self.nc = nc
        self.ops = []
        self.last_w = {}
        self.readers = {}
        self.sem_h = {}
        self.sem_cnt = {}
        self.last_eng = {}
        self.last_tag = {}
        self.barrier = {e: set() for e in ENGS}
        self.know = {e: {} for e in ENGS}
        self.tok_vc = {}
        self.nops = 0
        self.phase = 0

    def _sem(self, name):
        if name not in self.sem_h:
            self.sem_h[name] = self.nc.alloc_semaphore(name=name)
            self.sem_cnt[name] = 0
        return self.sem_h[name]

    def add(self, eng, fn, reads=(), writes=(), dma=False, tag=None, ninc=1):
        op = Op(eng, fn)
        op.dma = dma
        op.ninc = ninc
        if dma:
            op.tag = tag
            op.sem = "d_" + tag
        else:
            op.sem = "c_%s_%d" % (eng, self.phase % 3)
        for r in reads:
            w = self.last_w.get(r)
            if w is not None:
                op.raw.add(w)
        for k in writes:
            w = self.last_w.get(k)
            if w is not None:
                op.oth.add(w)
            for rd in self.readers.get(k, ()):
                op.oth.add(rd)
        if self.barrier[eng]:
            op.raw |= self.barrier[eng]
            self.barrier[eng] = set()
        for r in reads:
            self.readers.setdefault(r, []).append(op)
        for k in writes:
            self.last_w[k] = op
            self.readers[k] = []
        op.raw.discard(op)
        op.oth.discard(op)
        for d in op.raw:
            d.inc = True
        for d in op.oth:
            d.inc = True
        self.ops.append(op)
        if dma:
            self.last_tag[tag] = op
        else:
            self.last_eng[eng] = op
        return op

    def full_barrier(self):
        fr = set(self.last_eng.values()) | set(self.last_tag.values())
        for d in fr:
            d.inc = True
        for e in ENGS:
            self.barrier[e] = set(fr)
        self.last_w = {}
        self.readers = {}

    def flush(self, final=False):
        self.full_barrier()
        nc = self.nc
        for op in self.ops:
            if op.dma:
                op.inc = True
            if op.inc and op.val is None:
                self._sem(op.sem)
                self.sem_cnt[op.sem] += (16 * op.ninc) if op.dma else 1
                op.val = self.sem_cnt[op.sem]
        plan = {}
        for op in self.ops:
            e = op.eng
            K = self.know[e]
            need = {}
            for d in op.raw:
                if (not d.dma) and d.eng == e and e == "pe":
                    continue
                need[d.sem] = max(need.get(d.sem, 0), d.val)
            for d in op.oth:
                if (not d.dma) and d.eng == e and e == "pe":
                    continue
                need[d.sem] = max(need.get(d.sem, 0), d.val)
            own = "c_%s_" % e
            order = sorted(need.items(), key=lambda kv: kv[0].startswith(own))
            waits = []
            for s_, v in order:
                if K.get(s_, 0) >= v:
                    continue
                waits.append((s_, v))
                K[s_] = v
                vc = self.tok_vc.get((s_, v))
                if vc:
                    for s2, v2 in vc.items():
                        if K.get(s2, 0) < v2:
                            K[s2] = v2
            plan[op] = waits
            if op.inc:
                self.tok_vc[(op.sem, op.val)] = {k: v for k, v in K.items() if k.startswith("c_")}
        final_waits = []
        if final:
            K = self.know["sp"]
            fw = {}
            for e in ENGS:
                for d in self.barrier[e]:
                    fw[d.sem] = max(fw.get(d.sem, 0), d.val)
            final_waits = [(s_, v) for s_, v in fw.items() if K.get(s_, 0) < v]
        per = {e: [] for e in ENGS}
        for op in self.ops:
            per[op.eng].append(op)

        def emit(ename, eng):
            for op in per[ename]:
                for s_, v in plan[op]:
                    eng.wait_ge(self.sem_h[s_], v)
                ins = op.fn(eng)
                if op.inc:
                    if op.dma:
                        if not isinstance(ins, (list, tuple)):
                            ins = [ins]
                        assert len(ins) == op.ninc
                        for i_ in ins:
                            i_.then_inc(self.sem_h[op.sem], 16)
                    else:
                        ins.then_inc(self.sem_h[op.sem], 1)
            if final and ename == "sp":
                for s_, v in final_waits:
                    eng.wait_ge(self.sem_h[s_], v)

        with nc.Block() as block:
            @block.tensor
            def _(e):
                emit("pe", e)

            @block.scalar
            def _(e):
                emit("act", e)

            @block.vector
            def _(e):
                emit("dve", e)

            @block.gpsimd
            def _(e):
                emit("pool", e)

            @block.sync
            def _(e):
                emit("sp", e)
        self.nops += len(self.ops)
        self.nwaits = getattr(self, "nwaits", 0) + sum(len(w) for w in plan.values())
        self.ops = []
        self.phase += 1


def make_consts(S):
    bf = ml_dtypes.bfloat16
    i = np.arange(128)
    c = {}
    c["ident"] = np.eye(128, dtype=np.float32).astype(bf)
    c["lneg"] = (-(i[:, None] >= i[None, :]).astype(np.float32)).astype(bf)
    c["oneg"] = (-np.ones((128, 128), np.float32)).astype(bf)
    c["negmask"] = (np.where(i[:, None] >= i[None, :], -30000.0, 0.0).astype(np.float32)).astype(bf)
    c["caus01"] = ((i[None, :] >= i[:, None]).astype(np.float32)).astype(bf)
    dk = 256
    omega = (1.0 / (np.float32(10000.0) ** np.linspace(0.0, 1.0, dk // 2, dtype=np.float32))).astype(np.float32)
    ang = (np.arange(S, dtype=np.float32)[:, None] * omega[None, :]).astype(np.float32)
    c["cosT"] = np.ascontiguousarray(np.cos(ang.astype(np.float64)).T).astype(np.float32)
    c["sinT"] = np.ascontiguousarray(np.sin(ang.astype(np.float64)).T).astype(np.float32)
    lg = np.log1p(-np.exp2(np.linspace(-5.0, -9.0, 4)))
    idx = np.arange(128, dtype=np.float64)
    qf = np.exp((idx[None, :] + 1.0) * lg[:, None])
    kf = np.exp(-(idx[None, :] + 1.0) * lg[:, None]) / 16.0
    c["qfac"] = qf.reshape(1, 512).astype(np.float32)
    c["kfac"] = kf.reshape(1, 512).astype(np.float32)
    cdec = [float(np.exp(128.0 * lg[h])) for h in range(4)]
    return c, cdec


def build_program(S, NSEQ, layers):
    NTOK = S * NSEQ
    NT = NTOK // 128
    NG = NTOK // 512
    TPS = S // 128
    nc = bass.Bass("TRN2", target_bir_lowering=False)
    _, cdec = make_consts(128)

    def din(name, shape, dt=F32):
        return nc.dram_tensor(name, shape, dt, kind="ExternalInput").ap()

    x_in = din("x", [NTOK, D])
    cT_in = din("cT", [128, NSEQ * 8])
    w_in = din("w_in", [DEPTH, D, DP])
    w_out = din("w_out", [DEPTH, DI, D])
    w_mod = din("w_mod", [DEPTH, D, 3 * D])
    b_mod = din("b_mod", [DEPTH, 3 * D])
    pre_norm = din("pre_norm", [DEPTH, D])
    post_norm = din("post_norm", [DEPTH, D])
    gncol_d = din("gn_col", [2, 128, 16])
    ident_d = din("ident", [128, 128], BF16)
    lneg_d = din("lneg", [128, 128], BF16)
    oneg_d = din("oneg", [128, 128], BF16)
    negmask_d = din("negmask", [128, 128], BF16)
    caus_d = din("caus01", [128, 128], BF16)
    cos_d = din("cosT", [128, S])
    sin_d = din("sinT", [128, S])
    qfac_d = din("qfac", [1, 512])
    kfac_d = din("kfac", [1, 512])
    out = nc.dram_tensor("out", [NTOK, D], F32, kind="ExternalOutput").ap()
    qT = nc.dram_tensor("s_qT", [D, NTOK], BF16).ap()
    kT = nc.dram_tensor("s_kT", [D, NTOK], BF16).ap()
    gT = nc.dram_tensor("s_gT", [DI, NTOK], BF16).ap()
    vS = nc.dram_tensor("s_v", [NTOK, DI], BF16).ap()
    ogT = nc.dram_tensor("s_ogT", [DI, NTOK], BF16).ap()

    P = Prog(nc)
    top = ExitStack()

    uid = [0]

    def sb(name, shape, dt, stack=None):
        uid[0] += 1
        return (stack or top).enter_context(nc.sbuf_tensor("sb%d_%s" % (uid[0], name), shape, dt))

    def pt(name, shape, dt, stack):
        uid[0] += 1
        return stack.enter_context(nc.psum_tensor("ps%d_%s" % (uid[0], name), shape, dt))

    ident = sb("ident", [128, 128], BF16)
    lneg = sb("lneg", [128, 128], BF16)
    oneg = sb("oneg", [128, 128], BF16)
    negmask = sb("negmask", [128, 128], BF16)
    caus = sb("caus", [128, 128], BF16)
    qfac = sb("qfac", [128, 512], F32)
    kfac = sb("kfac", [128, 512], F32)
    cb = [sb(f"cb{b}", [128, 8, 128], BF16) for b in range(NSEQ)]
    AM = [sb(f"AM{b}", [128, D], F32) for b in range(NSEQ)]
    SH = [sb(f"SH{b}", [128, D], F32) for b in range(NSEQ)]
    GM = [sb(f"GM{b}", [128, D], F32) for b in range(NSEQ)]

    def dma(eng, out_ap, in_ap, reads, writes, tag):
        P.add(eng, lambda g: g.dma_start(out=out_ap, in_=in_ap), reads=reads, writes=writes, dma=True, tag=tag)

    def mmgroup(items, reads, writes):
        def fn(g):
            ins = None
            for (o, l, r, st, sp_, sk) in items:
                ins = g.matmul(o, lhsT=l, rhs=r, start=st, stop=sp_, skip_group_check=sk)
            return ins
        P.add("pe", fn, reads=reads, writes=writes)

    def trgroup(items, reads, writes):
        def fn(g):
            ins = None
            for (o, i_) in items:
                ins = g.transpose(out=o, in_=i_, identity=ident[:])
            return ins
        P.add("pe", fn, reads=list(reads) + ["ident"], writes=writes)

    def act(out_ap, in_ap, func, reads, writes, **kw):
        P.add("act", lambda g: g.activation(out=out_ap, in_=in_ap, func=func, **kw), reads=reads, writes=writes)

    def tt(eng, out_ap, a, b_, op, reads, writes):
        P.add(eng, lambda g: g.tensor_tensor(out=out_ap, in0=a, in1=b_, op=op), reads=reads, writes=writes)

    def ts(eng, out_ap, a, s1, s2, op0, op1, reads, writes):
        if op1 is None:
            P.add(eng, lambda g: g.tensor_scalar(out=out_ap, in0=a, scalar1=s1, scalar2=None, op0=op0),
                  reads=reads, writes=writes)
        else:
            P.add(eng, lambda g: g.tensor_scalar(out=out_ap, in0=a, scalar1=s1, scalar2=s2, op0=op0, op1=op1),
                  reads=reads, writes=writes)

    def stt(eng, out_ap, a, s, b_, op0, op1, reads, writes):
        P.add(eng, lambda g: g.scalar_tensor_tensor(out=out_ap, in0=a, scalar=s, in1=b_, op0=op0, op1=op1),
              reads=reads, writes=writes)

    def rstd_chain(ssq, rs, key_in, key_out, scale, eps):
        ts("dve", rs, ssq, scale, eps, ALU.mult, ALU.add, [key_in], [key_out])
        act(rs, rs, AF.Ln, [key_out], [key_out])
        act(rs, rs, AF.Exp, [key_out], [key_out], scale=-0.5)

    with ExitStack() as es:
        ct = sb("ct", [128, NSEQ * 8], F32, es)
        cs = sb("cs", [128, NSEQ * 8], F32, es)
        ones = sb("ones", [128, 128], F32, es)
        for nm, t_, d_ in (("ident", ident, ident_d), ("lneg", lneg, lneg_d), ("oneg", oneg, oneg_d),
                           ("negmask", negmask, negmask_d), ("caus", caus, caus_d)):
            dma("sp", t_[:], d_, [], [nm], "c_" + nm)
        dma("sp", qfac[:], qfac_d.partition_broadcast(128), [], ["qfac"], "c_qfac")
        dma("sp", kfac[:], kfac_d.partition_broadcast(128), [], ["kfac"], "c_kfac")
        dma("sp", ct[:], cT_in, [], ["ct"], "c_ct")
        P.add("pool", lambda g: g.memset(ones[:], 1.0), writes=["ones"])
        act(cs[:], ct[:], AF.Silu, ["ct"], ["cs"])
        for b in range(NSEQ):
            for kc in range(8):
                ts("dve", cb[b][:, kc, :], ones[:], cs[:, b * 8 + kc:b * 8 + kc + 1], None, ALU.mult, None,
                   ["ones", "cs"], [("cb", b)])
        P.flush()

    def phase_M(l, win):
        with ExitStack() as es:
            wm = sb("wm", [128, 8, 3 * D], BF16, es)
            bmod = sb("bmod", [128, 3 * D], F32, es)
            pre = sb("pre", [128, D], F32, es)
            post = sb("post", [128, D], F32, es)
            ps = pt("psM", [128, 6, 512], F32, es)
            dma("sp", bmod[:], b_mod[l:l + 1, :].partition_broadcast(128), [], ["bmod"], "bmod")
            dma("sp", pre[:], pre_norm[l:l + 1, :].partition_broadcast(128), [], ["pre"], "pre")
            dma("sp", post[:], post_norm[l:l + 1, :].partition_broadcast(128), [], ["post"], "post")
            for kc in range(8):
                dma("pool", wm[:, kc, :], w_mod[l, kc * 128:(kc + 1) * 128, :], [], [("wm", kc)], f"wm{kc % 2}")
            for kc in range(8):
                dma("pool", win[:, kc, :], w_in[l, kc * 128:(kc + 1) * 128, :], [], [("win", kc)], f"win{kc}")
            for b in range(NSEQ):
                items = [(ps[:, n, :], cb[b][:, kc, :], wm[:, kc, n * 512:(n + 1) * 512], kc == 0, kc == 7, False)
                         for n in range(6) for kc in range(8)]
                mmgroup(items, [("wm", kc) for kc in range(8)] + [("cb", b)], ["psM"])
                psv = ps[:, :, :].rearrange("p a b -> p (a b)")
                tt("dve", SH[b][:], psv[:, 0:D], bmod[:, 0:D], ALU.add, ["psM", "bmod"], [("SH", b)])
                tt("dve", AM[b][:], psv[:, D:2 * D], bmod[:, D:2 * D], ALU.add, ["psM", "bmod"], [("AM", b)])
                tt("dve", GM[b][:], psv[:, 2 * D:3 * D], bmod[:, 2 * D:3 * D], ALU.add, ["psM", "bmod"], [("GM", b)])
                stt("dve", AM[b][:], AM[b][:], 1.0, pre[:], ALU.add, ALU.mult, [("AM", b), "pre"], [("AM", b)])
                tt("pool", GM[b][:], GM[b][:], post[:], ALU.mult, [("GM", b), "post"], [("GM", b)])
            P.flush()

    def phase_A(l, xsrc, qscale, win):
        with ExitStack() as es:
            xt = [sb(f"xtA{i}", [128, D], F32, es) for i in range(4)]
            junk = sb("junkA", [128, D], BF16, es)
            ssq = [sb(f"ssqA{i}", [128, 1], F32, es) for i in range(4)]
            rs = [sb(f"rsA{i}", [128, 1], F32, es) for i in range(4)]
            t1 = [sb(f"t1A{i}", [128, D], F32, es) for i in range(2)]
            hb = [sb(f"hbA{i}", [128, D], BF16, es) for i in range(4)]
            hT = [sb(f"hTA{i}", [128, 8, 512], BF16, es) for i in range(2)]
            stf = [sb(f"stfA{i}", [128, 512], BF16, es) for i in range(4)]
            stv = [sb(f"stvA{i}", [128, DI], BF16, es) for i in range(2)]
            pst = pt("pstA", [128, 2, 1024], BF16, es)
            ps = pt("psA", [128, 6, 512], F32, es)
            winkeys = [("win", kc) for kc in range(8)]
            cnt = dict(bank=0, nst=0, nsv=0)

            def prepA(g):
                Ts = [g * 4 + t4 for t4 in range(4)]
                for T in Ts:
                    i4 = T % 4
                    dma("pool", xt[i4][:], xsrc[T * 128:(T + 1) * 128, :], [], [("xt", i4)], f"xtA{i4}")
                for T in Ts:
                    i4 = T % 4
                    act(junk[:], xt[i4][:], AF.Square, [("xt", i4)], ["junk", ("ssq", i4)], accum_out=ssq[i4][:])
                for T in Ts:
                    i4 = T % 4
                    ts("dve", rs[i4][:], ssq[i4][:], 1.0 / D, RMS_EPS, ALU.mult, ALU.add, [("ssq", i4)], [("rs", i4)])
                for T in Ts:
                    i4 = T % 4
                    act(rs[i4][:], rs[i4][:], AF.Ln, [("rs", i4)], [("rs", i4)])
                for T in Ts:
                    i4 = T % 4
                    act(rs[i4][:], rs[i4][:], AF.Exp, [("rs", i4)], [("rs", i4)], scale=-0.5)
                for T in Ts:
                    b = (T * 128) // S
                    i2 = T % 2
                    i4 = T % 4
                    stt("dve", t1[i2][:], xt[i4][:], rs[i4][:, 0:1], AM[b][:], ALU.mult, ALU.mult,
                        [("xt", i4), ("rs", i4), ("AM", b)], [("t1", i2)])
                    tt("pool", hb[i4][:], t1[i2][:], SH[b][:], ALU.add, [("t1", i2), ("SH", b)], [("hb", i4)])

            def prepB(g):
                hk = ("hT", g % 2)
                for t4 in range(4):
                    T = g * 4 + t4
                    i2 = T % 2
                    i4 = T % 4
                    trgroup([(pst[:, i2, kc * 128:(kc + 1) * 128], hb[i4][:, kc * 128:(kc + 1) * 128]) for kc in range(8)],
                            [("hb", i4)], [("pst", i2)])
                    act(hT[g % 2][:, :, t4 * 128:(t4 + 1) * 128], pst[:, i2, :].rearrange("p (a b) -> p a b", a=8),
                        AF.Copy, [("pst", i2)], [("hT", g % 2, t4)])

            def projF(g):
                hk = ("hT", g % 2)
                fm = [(qT, n, n * 128, "q") for n in range(8)] + [(kT, n, D + n * 128, "k") for n in range(8)] + \
                     [(gT, n, 2 * D + DI + n * 128, "g") for n in range(16)]
                for (dst, n, col, kind) in fm:
                    bk = cnt["bank"] % 6
                    cnt["bank"] += 1
                    items = [(ps[:, bk, :], win[:, kc, col:col + 128], hT[g % 2][:, kc, :], kc == 0, kc == 7, False)
                             for kc in range(8)]
                    mmgroup(items, winkeys + [("hT", g % 2, q4) for q4 in range(4)], [("ps", bk)])
                    si = cnt["nst"] % 4
                    cnt["nst"] += 1
                    if kind == "q":
                        ts("dve", stf[si][:], ps[:, bk, :], float(qscale), None, ALU.mult, None, [("ps", bk)], [("stf", si)])
                    elif kind == "k":
                        P.add("dve", lambda g_, si=si, bk=bk: g_.tensor_copy(out=stf[si][:], in_=ps[:, bk, :]),
                              reads=[("ps", bk)], writes=[("stf", si)])
                    else:
                        act(stf[si][:], ps[:, bk, :], AF.Silu, [("ps", bk)], [("stf", si)])
                    dma("sp", dst[n * 128:(n + 1) * 128, g * 512:(g + 1) * 512], stf[si][:], [("stf", si)], [], f"stfA{si}")

            def projV(g):
                hk = ("hT", g % 2)
                for t4 in range(4):
                    T = g * 4 + t4
                    sv = cnt["nsv"] % 2
                    cnt["nsv"] += 1
                    for nv in range(4):
                        bk = cnt["bank"] % 6
                        cnt["bank"] += 1
                        col = 2 * D + nv * 512
                        items = [(ps[:, bk, :], hT[g % 2][:, kc, t4 * 128:(t4 + 1) * 128], win[:, kc, col:col + 512],
                                  kc == 0, kc == 7, False) for kc in range(8)]
                        mmgroup(items, winkeys + [("hT", g % 2, t4)], [("ps", bk)])
                        if nv % 2 == 0:
                            P.add("dve", lambda g_, sv=sv, bk=bk, nv=nv: g_.tensor_copy(
                                out=stv[sv][:, nv * 512:(nv + 1) * 512], in_=ps[:, bk, :]),
                                reads=[("ps", bk)], writes=[("stv", sv)])
                        else:
                            act(stv[sv][:, nv * 512:(nv + 1) * 512], ps[:, bk, :], AF.Copy, [("ps", bk)], [("stv", sv)])
                    dma("sp", vS[T * 128:(T + 1) * 128, :], stv[sv][:], [("stv", sv)], [], f"stvA{sv}")

            prepA(0)
            prepB(0)
            for g in range(NG):
                if g + 1 < NG:
                    prepA(g + 1)
                projF(g)
                if g + 1 < NG:
                    prepB(g + 1)
                projV(g)
            P.flush()

    def phase_B_sb(l, wo_pre=None):
        QW = 1024
        NQC = S // QW
        KPC = QW // 128
        with ExitStack() as es:
            qh = [sb(f"qh{i}", [128, S], BF16, es) for i in range(2)]
            kh = [sb(f"kh{i}", [128, S], BF16, es) for i in range(2)]
            vh = [sb(f"vh{i}", [128, TPS, 128], BF16, es) for i in range(2)]
            sgh = [sb(f"sgh{i}", [128, S], BF16, es) for i in range(2)]
            e_sb = [sb(f"e_sb{i}", [128, QW], F32, es) for i in range(4)]
            sp_sb = [sb(f"sp_sb{i}", [128, QW], BF16, es) for i in range(3)]
            p_sb = [sb(f"p_sb{i}", [128, QW], F32, es) for i in range(2)]
            a_sb = [sb(f"a_sb{i}", [128, QW], BF16, es) for i in range(3)]
            ssb = [sb(f"ssb{i}", [128, QW], BF16, es) for i in range(2)]
            ogs = [sb(f"ogsB{i}", [128, QW], BF16, es) for i in range(2)]
            pz = pt("pzB", [128, 4, 512], F32, es)
            pw = pt("pwB", [128, 2, 512], F32, es)
            po = pt("poB", [128, 2, 512], F32, es)
            pwf = pw[:, :, :].rearrange("p a b -> p (a b)")
            pof = po[:, :, :].rearrange("p a b -> p (a b)")
            if wo_pre is not None:
                for kc in range(16):
                    dma("pool", wo_pre[:, kc, :], w_out[l, kc * 128:(kc + 1) * 128, :], [], [("wo", kc)], f"wo{kc % 4}")
            for i in range(2):
                P.add("dve", lambda g, i=i: g.memset(qh[i][64:128, :], 0.0), writes=[("qh", i)])
                P.add("pool", lambda g, i=i: g.memset(kh[i][64:128, :], 0.0), writes=[("kh", i)])
            tiles = []
            hc = 0
            cc = 0
            for b in range(NSEQ):
                for h in range(16):
                    for qc in range(NQC):
                        tp = KPC * qc + KPC - 1
                        for kb in range(tp, -1, -1):
                            tiles.append(dict(b=b, h=h, qc=qc, kb=kb, j=tp - kb, top=tp, hc=hc, cc=cc,
                                              first=(qc == 0 and kb == tp)))
                        cc += 1
                    hc += 1
            n = len(tiles)

            def geom(t):
                kb, qc = t["kb"], t["qc"]
                diag = kb >= KPC * qc
                c0 = 128 * (kb - KPC * qc) if diag else 0
                segs = []
                if c0 < 512:
                    segs.append((0, c0, 512))
                segs.append((1, max(c0, 512), QW))
                return diag, c0, segs

            def load_head(hc_):
                if hc_ >= NSEQ * 16:
                    return
                b_, h_ = hc_ // 16, hc_ % 16
                p2 = hc_ % 2
                tok = slice(b_ * S, (b_ + 1) * S)
                dma("sp", qh[p2][0:64, :], qT[h_ * 64:(h_ + 1) * 64, tok], [], [("qh", p2)], f"qh{p2}")
                dma("sp", kh[p2][0:64, :], kT[h_ * 64:(h_ + 1) * 64, tok], [], [("kh", p2)], f"kh{p2}")
                dma("sp", vh[p2][:], vS[tok, h_ * 128:(h_ + 1) * 128].rearrange("(kb p) d -> p kb d", p=128),
                    [], [("vh", p2)], f"vh{p2}")
                dma("sp", sgh[p2][:], gT[h_ * 128:(h_ + 1) * 128, tok], [], [("sgh", p2)], f"sgh{p2}")

            def prefetch(i):
                t = tiles[i]
                if t["qc"] == 0 and t["kb"] == 0:
                    load_head(t["hc"] + 1)

            def st1(i):
                t = tiles[i]
                qc, kb = t["qc"], t["kb"]
                s2 = t["hc"] % 2
                if t["first"] and t["hc"] == 0:
                    load_head(0)
                diag, c0, segs = geom(t)
                t0 = qc * QW
                pp = i % 2
                items = []
                rd = [("qh", s2), ("kh", s2)]
                for (bk, lo, hi) in segs:
                    has_mask = diag and lo <= c0 < hi
                    items.append((pz[:, 2 * pp + bk, lo - 512 * bk:hi - 512 * bk], kh[s2][:, kb * 128:(kb + 1) * 128],
                                  qh[s2][:, t0 + lo:t0 + hi], True, not has_mask, False))
                    if has_mask:
                        items.append((pz[:, 2 * pp + bk, c0 - 512 * bk:c0 - 512 * bk + 128], ident[:], negmask[:],
                                      False, True, True))
                if diag:
                    rd += ["ident", "negmask"]
                mmgroup(items, rd, [("pz", pp)])
                pzf = pz[:, 2 * pp:2 * pp + 2, :].rearrange("p a b -> p (a b)")
                act(e_sb[i % 4][:, c0:QW], pzf[:, c0:QW], AF.Exp, [("pz", pp)], [("e", i % 4)])

            def st1b(i):
                t = tiles[i]
                diag, c0, segs = geom(t)
                act(sp_sb[i % 3][:, c0:QW], e_sb[i % 4][:, c0:QW], AF.Ln, [("e", i % 4)], [("sp", i % 3)], bias=1.0)

            def st2(i):
                t = tiles[i]
                kb, j = t["kb"], t["j"]
                diag, c0, segs = geom(t)
                y = i % 3
                w2 = i % 2
                items = []
                rd = [("sp", y), "lneg"]
                for (bk, lo, hi) in segs:
                    items.append((pw[:, bk, lo - 512 * bk:hi - 512 * bk], lneg[:], sp_sb[y][:, lo:hi], True, j == 0, False))
                    if j > 0:
                        items.append((pw[:, bk, lo - 512 * bk:hi - 512 * bk], oneg[:], ssb[j % 2][:, lo:hi], False, True, False))
                if j > 0:
                    rd += ["oneg", ("ss", j % 2), ("ssz", j % 2)]
                mmgroup(items, rd, ["pw"])
                if kb > 0:
                    nx = (j + 1) % 2
                    e = "pool" if i % 2 == 0 else "dve"
                    if c0 > 0:
                        P.add(e, lambda g: g.memset(ssb[nx][:, 0:c0], 0.0), writes=[("ssz", nx)])
                    if j == 0:
                        P.add(e, lambda g: g.tensor_copy(out=ssb[nx][:, c0:QW], in_=sp_sb[y][:, c0:QW]),
                              reads=[("sp", y)], writes=[("ss", nx)])
                    else:
                        tt(e, ssb[nx][:, c0:QW], ssb[j % 2][:, c0:QW], sp_sb[y][:, c0:QW], ALU.add,
                           [("ss", j % 2), ("sp", y)], [("ss", nx)])
                act(p_sb[w2][:, c0:QW], pwf[:, c0:QW], AF.Exp, ["pw"], [("p", w2)])

            def st3(i):
                t = tiles[i]
                diag, c0, segs = geom(t)
                tt("dve", a_sb[i % 3][:, c0:QW], e_sb[i % 4][:, c0:QW], p_sb[i % 2][:, c0:QW], ALU.mult,
                   [("e", i % 4), ("p", i % 2)], [("a", i % 3)])

            def st4(i):
                t = tiles[i]
                b, h, qc, kb, j = t["b"], t["h"], t["qc"], t["kb"], t["j"]
                s2 = t["hc"] % 2
                diag, c0, segs = geom(t)
                items = []
                for (bk, lo, hi) in segs:
                    first = (j == 0) if bk == 1 else (j == KPC // 2)
                    items.append((po[:, bk, lo - 512 * bk:hi - 512 * bk], vh[s2][:, kb, :], a_sb[i % 3][:, lo:hi],
                                  first, kb == 0, True))
                mmgroup(items, [("vh", s2), ("a", i % 3)], ["po"])
                if kb == 0:
                    t0 = qc * QW
                    o2 = t["cc"] % 2
                    tt("dve", ogs[o2][:], pof, sgh[s2][:, t0:t0 + QW], ALU.mult, ["po", ("sgh", s2)], [("ogs", o2)])
                    dma("sp", ogT[h * 128:(h + 1) * 128, b * S + t0:b * S + t0 + QW], ogs[o2][:],
                        [("ogs", o2)], [], f"ogsB{o2}")

            for step in range(n + 4):
                if step < n:
                    st1(step)
                if 0 <= step - 1 < n:
                    st1b(step - 1)
                if 0 <= step - 2 < n:
                    st2(step - 2)
                if 0 <= step - 3 < n:
                    st3(step - 3)
                if 0 <= step - 4 < n:
                    st4(step - 4)
                if step < n:
                    prefetch(step)
            P.flush()

    def phase_B_ret(l):
        gi = l // 2
        GT = 256
        CH = GT // 128
        NGR = S // GT
        with ExitStack() as es:
            qg = [sb(f"qgR{i}", [128, 8, GT], BF16, es) for i in range(2)]
            kg = [sb(f"kgR{i}", [128, 8, GT], BF16, es) for i in range(2)]
            vg = [sb(f"vgR{i}", [128, CH, DI], BF16, es) for i in range(2)]
            sgg = [sb(f"sggR{i}", [128, 16, GT], BF16, es) for i in range(2)]
            cs_ = [sb(f"cosR{i}", [128, GT], F32, es) for i in range(2)]
            sn_ = [sb(f"sinR{i}", [128, GT], F32, es) for i in range(2)]
            ta = {(e, i): sb(f"taR{e}{i}", [128, GT], F32, es) for e in ("dve", "pool") for i in range(2)}
            tb = {(e, i): sb(f"tbR{e}{i}", [128, GT], F32, es) for e in ("dve", "pool") for i in range(2)}
            qr = [sb(f"qrR{i}", [128, 8, GT], BF16, es) for i in range(2)]
            kr = [sb(f"krR{i}", [128, 8, GT], BF16, es) for i in range(2)]
            ktok = sb("ktokR", [128, CH, D], BF16, es)
            st = [sb(f"stR{h}", [128, 2, 512], F32, es) for h in range(4)]
            stb = [sb(f"stbR{h}", [128, 2, 512], BF16, es) for h in range(4)]
            scm = [sb(f"scmR{i}", [128, 4, 128], BF16, es) for i in range(2)]
            gncol = sb("gncolR", [128, 16], F32, es)
            bst = [sb(f"bstR{i}", [128, 6], F32, es) for i in range(4)]
            mv = [sb(f"mvR{i}", [128, 2], F32, es) for i in range(4)]
            rsg = [sb(f"rsgR{i}", [128, 1], F32, es) for i in range(4)]
            ob = [sb(f"obR{i}", [128, DI], BF16, es) for i in range(2)]
            ogs = sb("ogsR", [128, 16, GT], BF16, es)
            psf = pt("psfR", [128, 6, 512], F32, es)
            psb = pt("psbR", [128, 2, 1024], BF16, es)
            dma("sp", gncol[:], gncol_d[gi], [], ["gncol"], "gnb")
            groups = [(b, g4) for b in range(NSEQ) for g4 in range(NGR)]
            NGRP = len(groups)

            def load(gx):
                b, g4 = groups[gx]
                z = gx % 2
                p0 = g4 * GT
                tk = slice(b * S + p0, b * S + p0 + GT)
                dma("sp", qg[z][:], qT[:, tk].rearrange("(fc p) t -> p fc t", p=128), [], [("qg", z)], f"qgR{z}")
                dma("sp", kg[z][:], kT[:, tk].rearrange("(fc p) t -> p fc t", p=128), [], [("kg", z)], f"kgR{z}")
                dma("sp", cs_[z][:], cos_d[:, p0:p0 + GT], [], [("cos", z)], f"cosR{z}")
                dma("sp", sn_[z][:], sin_d[:, p0:p0 + GT], [], [("sin", z)], f"sinR{z}")
                dma("sp", vg[z][:], vS[tk, :].rearrange("(tt p) d -> p tt d", p=128), [], [("vg", z)], f"vgR{z}")
                dma("sp", sgg[z][:], gT[:, tk].rearrange("(fc p) t -> p fc t", p=128), [], [("sgg", z)], f"sggR{z}")

            def rope_units(gx):
                z = gx % 2
                units = []
                k_ = 0
                ecnt = {}
                for h in range(4):
                    for (src, skey, dst, dkey, fac, fkey) in ((qg[z], ("qg", z), qr[z], "qr", qfac, "qfac"),
                                                              (kg[z], ("kg", z), kr[z], "kr", kfac, "kfac")):
                        for half in range(2):
                            en_ = "dve" if k_ % 5 in (0, 2) else "pool"
                            ecnt[en_] = ecnt.get(en_, 0) + 1
                            e = (en_, ecnt[en_] % 2)
                            k_ += 1

                            def unit(e=e, h=h, src=src, skey=skey, dst=dst, dkey=dkey, fac=fac, fkey=fkey, half=half):
                                fbc = fac[:, h * 128:(h + 1) * 128].unsqueeze(1).broadcast_to([128, CH, 128])
                                xa = src[:, 2 * h + half, :]
                                xb_ = src[:, 2 * h + 1 - half, :]
                                tak, tbk = ("ta", e), ("tb", e)
                                en = e[0]
                                tt(en, ta[e][:], xa, cs_[z][:], ALU.mult, [skey, ("cos", z)], [tak])
                                tt(en, tb[e][:], xb_, sn_[z][:], ALU.mult, [skey, ("sin", z)], [tbk])
                                tt(en, ta[e][:], ta[e][:], tb[e][:], ALU.subtract if half == 0 else ALU.add,
                                   [tak, tbk], [tak])
                                tt(en, dst[:, 2 * h + half, :].rearrange("p (a b) -> p a b", a=CH),
                                   ta[e][:].rearrange("p (a b) -> p a b", a=CH), fbc, ALU.mult,
                                   [tak, fkey], [(dkey, z, 2 * h + half)])
                            units.append(unit)
                return units

            cnts = dict(o=0, c=0)

            def gnfold(gx):
                z = gx % 2
                for fc in range(16):
                    tt("pool", sgg[z][:, fc, :], sgg[z][:, fc, :], gncol[:, fc:fc + 1].broadcast_to([128, GT]), ALU.mult,
                       [("sgg", z), "gncol"], [("sggf", z, fc)])

            def chunks(gx, pending):
                b, g4 = groups[gx]
                z = gx % 2
                p0 = g4 * GT
                tk = slice(b * S + p0, b * S + p0 + GT)
                if g4 == 0:
                    for h in range(4):
                        P.add("pool", lambda g, h=h: g.memset(st[h][:], 0.0), writes=[("st", h)])
                        P.add("pool", lambda g, h=h: g.memset(stb[h][:], 0.0), writes=[("stb", h)])
                if gx == 0:
                    gnfold(0)
                if gx + 1 < NGRP:
                    gnfold(gx + 1)
                krkeys = [("kr", z, i) for i in range(8)]
                qrkeys = [("qr", z, i) for i in range(8)]
                for c in range(CH):
                    tsl = slice(c * 128, (c + 1) * 128)
                    trgroup([(psb[:, 0, fc * 128:(fc + 1) * 128], kr[z][:, fc, tsl]) for fc in range(8)],
                            krkeys, [("psb", 0)])
                    for h in range(4):
                        act(ktok[:, c, h * 256:(h + 1) * 256], psb[:, 0, h * 256:(h + 1) * 256], AF.Copy,
                            [("psb", 0)], [("ktok", c, h)], scale=cdec[h])
                for c in range(CH):
                    tsl = slice(c * 128, (c + 1) * 128)
                    items = []
                    for h in range(4):
                        for half in range(2):
                            items.append((psf[:, 0, h * 128:(h + 1) * 128], kr[z][:, 2 * h + half, tsl],
                                          qr[z][:, 2 * h + half, tsl], half == 0, half == 1, True))
                    mmgroup(items, krkeys + qrkeys, [("psf", 0)])
                    cc = cnts["c"]
                    cnts["c"] += 1
                    sc = scm[cc % 2]
                    sck = ("scm", cc % 2)
                    tt("dve", sc[:], psf[:, 0, :].rearrange("p (a b) -> p a b", a=4),
                       caus[:].unsqueeze(1).broadcast_to([128, 4, 128]), ALU.mult, [("psf", 0), "caus"], [sck])
                    obk = ("ob", cc % 2)
                    obt = ob[cc % 2]
                    o2s = {}

                    def o_mm(h):
                        o2 = 1 + cnts["o"] % 3
                        cnts["o"] += 1
                        o2s[h] = o2
                        vsl = vg[z][:, c, h * 512:(h + 1) * 512]
                        mmgroup([(psf[:, o2, :], sc[:, h, :], vsl, True, False, False),
                                 (psf[:, o2, :], qr[z][:, 2 * h, tsl], stb[h][:, 0, :], False, False, False),
                                 (psf[:, o2, :], qr[z][:, 2 * h + 1, tsl], stb[h][:, 1, :], False, True, False)],
                                [sck, ("vg", z), ("qr", z, 2 * h), ("qr", z, 2 * h + 1), ("stb", h)], [("psf", o2)])

                    def state_upd(h):
                        vsl = vg[z][:, c, h * 512:(h + 1) * 512]
                        mmgroup([(psf[:, 4 + half, :], ktok[:, c, h * 256 + half * 128:h * 256 + (half + 1) * 128],
                                  vsl, True, True, False) for half in range(2)],
                                [("ktok", c, h), ("vg", z)], [("psf", 4)])
                        stt("dve", st[h][:].rearrange("p a b -> p (a b)"), st[h][:].rearrange("p a b -> p (a b)"),
                            cdec[h], psf[:, 4:6, :].rearrange("p a b -> p (a b)"), ALU.mult, ALU.add,
                            [("st", h), ("psf", 4)], [("st", h)])
                        act(stb[h][:], st[h][:], AF.Copy, [("st", h)], [("stb", h)])

                    def gn_stats(h):
                        o2 = o2s[h]
                        P.add("dve", lambda g, h=h, o2=o2: g.bn_stats(out=bst[h][:], in_=psf[:, o2, :]),
                              reads=[("psf", o2)], writes=[("bst", h)])
                        P.add("dve", lambda g, h=h: g.bn_aggr(out=mv[h][:], in_=bst[h][:]),
                              reads=[("bst", h)], writes=[("mv", h)])
                        ts("dve", rsg[h][:], mv[h][:, 1:2], 1.0, GN_EPS, ALU.mult, ALU.add, [("mv", h)], [("rsg", h)])
                        act(rsg[h][:], rsg[h][:], AF.Ln, [("rsg", h)], [("rsg", h)])
                        act(rsg[h][:], rsg[h][:], AF.Exp, [("rsg", h)], [("rsg", h)], scale=-0.5)

                    def gn_norm(h):
                        o2 = o2s[h]
                        ts("dve", obt[:, h * 512:(h + 1) * 512], psf[:, o2, :], mv[h][:, 0:1], rsg[h][:, 0:1],
                           ALU.subtract, ALU.mult, [("psf", o2), ("mv", h), ("rsg", h)], [obk])

                    def fill(k):
                        for _ in range(k):
                            if pending:
                                pending.pop(0)()

                    o_mm(0)
                    o_mm(1)
                    o_mm(2)
                    state_upd(0)
                    gn_stats(0)
                    state_upd(1)
                    gn_stats(1)
                    gn_norm(0)
                    o_mm(3)
                    state_upd(2)
                    gn_stats(2)
                    gn_norm(1)
                    state_upd(3)
                    gn_stats(3)
                    gn_norm(2)
                    fill(16 // CH)
                    gn_norm(3)
                    for hf in range(2):
                        trgroup([(psb[:, hf, fc * 128:(fc + 1) * 128], obt[:, (hf * 8 + fc) * 128:(hf * 8 + fc + 1) * 128])
                                 for fc in range(8)], [obk], [("psb", hf)])
                        tt("dve", ogs[:, hf * 8:(hf + 1) * 8, tsl], psb[:, hf, :].rearrange("p (a b) -> p a b", a=8),
                           sgg[z][:, hf * 8:(hf + 1) * 8, tsl], ALU.mult,
                           [("psb", hf), ("sgg", z)] + [("sggf", z, hf * 8 + q) for q in range(8)], ["ogs"])
                while pending:
                    pending.pop(0)()
                dma("sp", ogT[:, tk].rearrange("(fc p) t -> p fc t", p=128), ogs[:], ["ogs"], [], "ogsR")

            load(0)
            if NGRP > 1:
                load(1)
            for u in rope_units(0):
                u()
            for gx in range(NGRP):
                pending = rope_units(gx + 1) if gx + 1 < NGRP else []
                chunks(gx, pending)
                if gx + 2 < NGRP:
                    load(gx + 2)
            P.flush()

    def phase_C(l, xsrc, wo_pre=None):
        with ExitStack() as es:
            wo = wo_pre if wo_pre is not None else sb("woC", [128, 16, D], BF16, es)
            ogg = [sb(f"oggC{i}", [128, 16, 512], BF16, es) for i in range(2)]
            xt = [sb(f"xtC{i}", [128, D], F32, es) for i in range(2)]
            junk = sb("junkC", [128, D], BF16, es)
            ssq = [sb(f"ssqC{i}", [128, 1], F32, es) for i in range(2)]
            rs = [sb(f"rsC{i}", [128, 1], F32, es) for i in range(2)]
            t1 = [sb(f"t1C{i}", [128, D], F32, es) for i in range(2)]
            xo = [sb(f"xoC{i}", [128, D], F32, es) for i in range(2)]
            py = pt("pyC", [128, 4, 512], F32, es)
            if wo_pre is None:
                for kc in range(16):
                    dma("pool", wo[:, kc, :], w_out[l, kc * 128:(kc + 1) * 128, :], [], [("wo", kc)], f"wo{kc % 4}")
            wokeys = [("wo", kc) for kc in range(16)]
            def ld_ogg(g):
                dma("sp", ogg[g % 2][:], ogT[:, g * 512:(g + 1) * 512].rearrange("(kc p) t -> p kc t", p=128),
                    [], [("ogg", g % 2)], f"oggC{g % 2}")

            def ld_xt(T):
                dma("sp", xt[T % 2][:], xsrc[T * 128:(T + 1) * 128, :], [], [("xt", T % 2)], f"xtC{T % 2}")

            ld_ogg(0)
            ld_xt(0)
            for g in range(NG):
                g2 = g % 2
                if g + 1 < NG:
                    ld_ogg(g + 1)
                for t4 in range(4):
                    T = g * 4 + t4
                    b = (T * 128) // S
                    i2 = T % 2
                    if T + 1 < NT:
                        ld_xt(T + 1)
                    items = []
                    for hf in range(2):
                        for kc in range(16):
                            items.append((py[:, 2 * i2 + hf, :], ogg[g2][:, kc, t4 * 128:(t4 + 1) * 128],
                                          wo[:, kc, hf * 512:(hf + 1) * 512], kc == 0, kc == 15, False))
                    mmgroup(items, wokeys + [("ogg", g2)], [("py", i2)])
                    yv = py[:, 2 * i2:2 * i2 + 2, :].rearrange("p a b -> p (a b)")
                    act(junk[:], yv, AF.Square, [("py", i2)], ["junk", ("ssq", i2)], accum_out=ssq[i2][:])
                    rstd_chain(ssq[i2][:], rs[i2][:], ("ssq", i2), ("rs", i2), 1.0 / D, RMS_EPS)
                    stt("dve", t1[i2][:], yv, rs[i2][:, 0:1], GM[b][:], ALU.mult, ALU.mult,
                        [("py", i2), ("rs", i2), ("GM", b)], [("t1", i2)])
                    tt("pool", xo[i2][:], t1[i2][:], xt[i2][:], ALU.add, [("t1", i2), ("xt", i2)], [("xo", i2)])
                    dma("sp", out[T * 128:(T + 1) * 128, :], xo[i2][:], [("xo", i2)], [], f"xoC{i2}")
            P.flush()

    first = True
    for l in layers:
        xsrc = x_in if first else out
        first = False
        with ExitStack() as esw:
            win_l = sb("win", [128, 8, DP], BF16, esw)
            phase_M(l, win_l)
            phase_A(l, xsrc, 1.0 if l % 2 == 0 else 0.125, win_l)
        if l % 2 == 0:
            phase_B_ret(l)
            phase_C(l, xsrc)
        else:
            with ExitStack() as es2:
                wo_pre = sb("woP", [128, 16, D], BF16, es2)
                phase_B_sb(l, wo_pre)
                phase_C(l, xsrc, wo_pre)
    P.add("sp", lambda g: g.nop(), reads=[], writes=[])
    P.flush(final=True)
    top.close()
    return nc, P


_CACHE = {}


def _get_program():
    if "nc" not in _CACHE:
        _CACHE["nc"] = build_program(SEQ, BATCH // NCORES, list(range(DEPTH)))[0]
    return _CACHE["nc"]


def make_in_maps(x, c, w_in, w_out, w_mod, b_mod, pre_norm, post_norm, ret_gn, S, nseq, ncores):
    consts, _ = make_consts(S)
    f = lambda a: np.ascontiguousarray(np.asarray(a, dtype=np.float32))
    shared = {"w_in": f(w_in), "w_out": f(w_out), "w_mod": f(w_mod), "b_mod": f(b_mod),
              "pre_norm": f(pre_norm), "post_norm": f(post_norm),
              "gn_col": np.ascontiguousarray(f(ret_gn).reshape(2, 16, 128).transpose(0, 2, 1))}
    shared.update(consts)
    x = f(x)
    c = f(c)
    maps = []
    for i in range(ncores):
        xs = x[i * nseq:(i + 1) * nseq].reshape(nseq * S, D)
        cc = c[i * nseq:(i + 1) * nseq]
        cT = np.ascontiguousarray(cc.reshape(nseq, 8, 128).transpose(2, 0, 1).reshape(128, nseq * 8))
        m = dict(shared)
        m["x"] = np.ascontiguousarray(xs)
        m["cT"] = cT
        maps.append(m)
    return maps


def kernel(x, c, w_in, w_out, w_mod, b_mod, pre_norm, post_norm, ret_gn):
    nseq = BATCH // NCORES
    nc = _get_program()
    maps = make_in_maps(x, c, w_in, w_out, w_mod, b_mod, pre_norm, post_norm, ret_gn, SEQ, nseq, NCORES)
    res = run_bass_kernel_spmd(nc, maps, core_ids=list(range(NCORES)))
    outs = [np.asarray(r["out"]).reshape(nseq, SEQ, D) for r in res.results]
    return np.concatenate(outs, axis=0).astype(np.float32)
```

```python
from contextlib import ExitStack

import numpy as np
import ml_dtypes
import concourse.bass as bass
import concourse.mybir as mybir
from concourse.bass_utils import run_bass_kernel_spmd

F32 = mybir.dt.float32
BF16 = mybir.dt.bfloat16
AF = mybir.ActivationFunctionType
ALU = mybir.AluOpType

D = 1024
DP = 6144
DI = 2048
DEPTH = 4
SEQ = 4096
BATCH = 16
NCORES = 8
RMS_EPS = 1e-6
GN_EPS = 1e-5
ENGS = ("pe", "act", "dve", "pool", "sp")


class Op:
    __slots__ = ("eng", "fn", "raw", "oth", "inc", "dma", "tag", "ninc", "sem", "val")

    def __init__(self, eng, fn):
        self.eng = eng
        self.fn = fn
        self.raw = set()
        self.oth = set()
        self.inc = False
        self.dma = False
        self.tag = None
        self.ninc = 1
        self.sem = None
        self.val = None


class Prog:
    def __init__(self, nc):
        self.nc = nc
        self.ops = []
        self.last_w = {}
        self.readers = {}
        self.sem_h = {}
        self.sem_cnt = {}
        self.last_eng = {}
        self.last_tag = {}
        self.barrier = {e: set() for e in ENGS}
        self.know = {e: {} for e in ENGS}
        self.tok_vc = {}
        self.nops = 0
        self.phase = 0

    def _sem(self, name):
        if name not in self.sem_h:
            self.sem_h[name] = self.nc.alloc_semaphore(name=name)
            self.sem_cnt[name] = 0
        return self.sem_h[name]

    def add(self, eng, fn, reads=(), writes=(), dma=False, tag=None, ninc=1):
        op = Op(eng, fn)
        op.dma = dma
        op.ninc = ninc
        if dma:
            op.tag = tag
            op.sem = "d_" + tag
        else:
            op.sem = "c_%s_%d" % (eng, self.phase % 3)
        for r in reads:
            w = self.last_w.get(r)
            if w is not None:
                op.raw.add(w)
        for k in writes:
            w = self.last_w.get(k)
            if w is not None:
                op.oth.add(w)
            for rd in self.readers.get(k, ()):
                op.oth.add(rd)
        if self.barrier[eng]:
            op.raw |= self.barrier[eng]
            self.barrier[eng] = set()
        for r in reads:
            self.readers.setdefault(r, []).append(op)
        for k in writes:
            self.last_w[k] = op
            self.readers[k] = []
        op.raw.discard(op)
        op.oth.discard(op)
        for d in op.raw:
            d.inc = True
        for d in op.oth:
            d.inc = True
        self.ops.append(op)
        if dma:
            self.last_tag[tag] = op
        else:
            self.last_eng[eng] = op
        return op

    def full_barrier(self):
        fr = set(self.last_eng.values()) | set(self.last_tag.values())
        for d in fr:
            d.inc = True
        for e in ENGS:
            self.barrier[e] = set(fr)
        self.last_w = {}
        self.readers = {}

    def flush(self, final=False):
        self.full_barrier()
        nc = self.nc
        for op in self.ops:
            if op.dma:
                op.inc = True
            if op.inc and op.val is None:
                self._sem(op.sem)
                self.sem_cnt[op.sem] += (16 * op.ninc) if op.dma else 1
                op.val = self.sem_cnt[op.sem]
        plan = {}
        for op in self.ops:
            e = op.eng
            K = self.know[e]
            need = {}
            for d in op.raw:
                if (not d.dma) and d.eng == e and e == "pe":
                    continue
                need[d.sem] = max(need.get(d.sem, 0), d.val)
            for d in op.oth:
                if (not d.dma) and d.eng == e and e == "pe":
                    continue
                need[d.sem] = max(need.get(d.sem, 0), d.val)
            own = "c_%s_" % e
            order = sorted(need.items(), key=lambda kv: kv[0].startswith(own))
            waits = []
            for s_, v in order:
                if K.get(s_, 0) >= v:
                    continue
                waits.append((s_, v))
                K[s_] = v
                vc = self.tok_vc.get((s_, v))
                if vc:
                    for s2, v2 in vc.items():
                        if K.get(s2, 0) < v2:
                            K[s2] = v2
            plan[op] = waits
            if op.inc:
                self.tok_vc[(op.sem, op.val)] = {k: v for k, v in K.items() if k.startswith("c_")}
        final_waits = []
        if final:
            K = self.know["sp"]
            fw = {}
            for e in ENGS:
                for d in self.barrier[e]:
                    fw[d.sem] = max(fw.get(d.sem, 0), d.val)
            final_waits = [(s_, v) for s_, v in fw.items() if K.get(s_, 0) < v]
        per = {e: [] for e in ENGS}
        for op in self.ops:
            per[op.eng].append(op)

        def emit(ename, eng):
            for op in per[ename]:
                for s_, v in plan[op]:
                    eng.wait_ge(self.sem_h[s_], v)
                ins = op.fn(eng)
                if op.inc:
                    if op.dma:
                        if not isinstance(ins, (list, tuple)):
                            ins = [ins]
                        assert len(ins) == op.ninc
                        for i_ in ins:
                            i_.then_inc(self.sem_h[op.sem], 16)
                    else:
                        ins.then_inc(self.sem_h[op.sem], 1)
            if final and ename == "sp":
                for s_, v in final_waits:
                    eng.wait_ge(self.sem_h[s_], v)

        with nc.Block() as block:
            @block.tensor
            def _(e):
                emit("pe", e)

            @block.scalar
            def _(e):
                emit("act", e)

            @block.vector
            def _(e):
                emit("dve", e)

            @block.gpsimd
            def _(e):
                emit("pool", e)

            @block.sync
            def _(e):
                emit("sp", e)
        self.nops += len(self.ops)
        self.nwaits = getattr(self, "nwaits", 0) + sum(len(w) for w in plan.values())
        self.ops = []
        self.phase += 1


def make_consts(S):
    bf = ml_dtypes.bfloat16
    i = np.arange(128)
    c = {}
    c["ident"] = np.eye(128, dtype=np.float32).astype(bf)
    c["lneg"] = (-(i[:, None] >= i[None, :]).astype(np.float32)).astype(bf)
    c["oneg"] = (-np.ones((128, 128), np.float32)).astype(bf)
    c["negmask"] = (np.where(i[:, None] >= i[None, :], -30000.0, 0.0).astype(np.float32)).astype(bf)
    c["caus01"] = ((i[None, :] >= i[:, None]).astype(np.float32)).astype(bf)
    dk = 256
    omega = (1.0 / (np.float32(10000.0) ** np.linspace(0.0, 1.0, dk // 2, dtype=np.float32))).astype(np.float32)
    ang = (np.arange(S, dtype=np.float32)[:, None] * omega[None, :]).astype(np.float32)
    c["cosT"] = np.ascontiguousarray(np.cos(ang.astype(np.float64)).T).astype(np.float32)
    c["sinT"] = np.ascontiguousarray(np.sin(ang.astype(np.float64)).T).astype(np.float32)
    lg = np.log1p(-np.exp2(np.linspace(-5.0, -9.0, 4)))
    idx = np.arange(128, dtype=np.float64)
    qf = np.exp((idx[None, :] + 1.0) * lg[:, None])
    kf = np.exp(-(idx[None, :] + 1.0) * lg[:, None]) / 16.0
    c["qfac"] = qf.reshape(1, 512).astype(np.float32)
    c["kfac"] = kf.reshape(1, 512).astype(np.float32)
    cdec = [float(np.exp(128.0 * lg[h])) for h in range(4)]
    return c, cdec


def build_program(S, NSEQ, layers):
    NTOK = S * NSEQ
    NT = NTOK // 128
    NG = NTOK // 512
    TPS = S // 128
    nc = bass.Bass("TRN2", target_bir_lowering=False)
    _, cdec = make_consts(128)

    def din(name, shape, dt=F32):
        return nc.dram_tensor(name, shape, dt, kind="ExternalInput").ap()

    x_in = din("x", [NTOK, D])
    cT_in = din("cT", [128, NSEQ * 8])
    w_in = din("w_in", [DEPTH, D, DP])
    w_out = din("w_out", [DEPTH, DI, D])
    w_mod = din("w_mod", [DEPTH, D, 3 * D])
    b_mod = din("b_mod", [DEPTH, 3 * D])
    pre_norm = din("pre_norm", [DEPTH, D])
    post_norm = din("post_norm", [DEPTH, D])
    gncol_d = din("gn_col", [2, 128, 16])
    ident_d = din("ident", [128, 128], BF16)
    lneg_d = din("lneg", [128, 128], BF16)
    oneg_d = din("oneg", [128, 128], BF16)
    negmask_d = din("negmask", [128, 128], BF16)
    caus_d = din("caus01", [128, 128], BF16)
    cos_d = din("cosT", [128, S])
    sin_d = din("sinT", [128, S])
    qfac_d = din("qfac", [1, 512])
    kfac_d = din("kfac", [1, 512])
    out = nc.dram_tensor("out", [NTOK, D], F32, kind="ExternalOutput").ap()
    qT = nc.dram_tensor("s_qT", [D, NTOK], BF16).ap()
    kT = nc.dram_tensor("s_kT", [D, NTOK], BF16).ap()
    gT = nc.dram_tensor("s_gT", [DI, NTOK], BF16).ap()
    vS = nc.dram_tensor("s_v", [NTOK, DI], BF16).ap()
    ogT = nc.dram_tensor("s_ogT", [DI, NTOK], BF16).ap()

    P = Prog(nc)
    top = ExitStack()

    uid = [0]

    def sb(name, shape, dt, stack=None):
        uid[0] += 1
        return (stack or top).enter_context(nc.sbuf_tensor("sb%d_%s" % (uid[0], name), shape, dt))

    def pt(name, shape, dt, stack):
        uid[0] += 1
        return stack.enter_context(nc.psum_tensor("ps%d_%s" % (uid[0], name), shape, dt))

    ident = sb("ident", [128, 128], BF16)
    lneg = sb("lneg", [128, 128], BF16)
    oneg = sb("oneg", [128, 128], BF16)
    negmask = sb("negmask", [128, 128], BF16)
    caus = sb("caus", [128, 128], BF16)
    qfac = sb("qfac", [128, 512], F32)
    kfac = sb("kfac", [128, 512], F32)
    cb = [sb(f"cb{b}", [128, 8, 128], BF16) for b in range(NSEQ)]
    AM = [sb(f"AM{b}", [128, D], F32) for b in range(NSEQ)]
    SH = [sb(f"SH{b}", [128, D], F32) for b in range(NSEQ)]
    GM = [sb(f"GM{b}", [128, D], F32) for b in range(NSEQ)]

    def dma(eng, out_ap, in_ap, reads, writes, tag):
        P.add(eng, lambda g: g.dma_start(out=out_ap, in_=in_ap), reads=reads, writes=writes, dma=True, tag=tag)

    def mmgroup(items, reads, writes):
        def fn(g):
            ins = None
            for (o, l, r, st, sp_, sk) in items:
                ins = g.matmul(o, lhsT=l, rhs=r, start=st, stop=sp_, skip_group_check=sk)
            return ins
        P.add("pe", fn, reads=reads, writes=writes)

    def trgroup(items, reads, writes):
        def fn(g):
            ins = None
            for (o, i_) in items:
                ins = g.transpose(out=o, in_=i_, identity=ident[:])
            return ins
        P.add("pe", fn, reads=list(reads) + ["ident"], writes=writes)

    def act(out_ap, in_ap, func, reads, writes, **kw):
        P.add("act", lambda g: g.activation(out=out_ap, in_=in_ap, func=func, **kw), reads=reads, writes=writes)

    def tt(eng, out_ap, a, b_, op, reads, writes):
        P.add(eng, lambda g: g.tensor_tensor(out=out_ap, in0=a, in1=b_, op=op), reads=reads, writes=writes)

    def ts(eng, out_ap, a, s1, s2, op0, op1, reads, writes):
        if op1 is None:
            P.add(eng, lambda g: g.tensor_scalar(out=out_ap, in0=a, scalar1=s1, scalar2=None, op0=op0),
                  reads=reads, writes=writes)
        else:
            P.add(eng, lambda g: g.tensor_scalar(out=out_ap, in0=a, scalar1=s1, scalar2=s2, op0=op0, op1=op1),
                  reads=reads, writes=writes)

    def stt(eng, out_ap, a, s, b_, op0, op1, reads, writes):
        P.add(eng, lambda g: g.scalar_tensor_tensor(out=out_ap, in0=a, scalar=s, in1=b_, op0=op0, op1=op1),
              reads=reads, writes=writes)

    def rstd_chain(ssq, rs, key_in, key_out, scale, eps):
        ts("dve", rs, ssq, scale, eps, ALU.mult, ALU.add, [key_in], [key_out])
        act(rs, rs, AF.Ln, [key_out], [key_out])
        act(rs, rs, AF.Exp, [key_out], [key_out], scale=-0.5)

    with ExitStack() as es:
        ct = sb("ct", [128, NSEQ * 8], F32, es)
        cs = sb("cs", [128, NSEQ * 8], F32, es)
        ones = sb("ones", [128, 128], F32, es)
        for nm, t_, d_ in (("ident", ident, ident_d), ("lneg", lneg, lneg_d), ("oneg", oneg, oneg_d),
                           ("negmask", negmask, negmask_d), ("caus", caus, caus_d)):
            dma("sp", t_[:], d_, [], [nm], "c_" + nm)
        dma("sp", qfac[:], qfac_d.partition_broadcast(128), [], ["qfac"], "c_qfac")
        dma("sp", kfac[:], kfac_d.partition_broadcast(128), [], ["kfac"], "c_kfac")
        dma("sp", ct[:], cT_in, [], ["ct"], "c_ct")
        P.add("pool", lambda g: g.memset(ones[:], 1.0), writes=["ones"])
        act(cs[:], ct[:], AF.Silu, ["ct"], ["cs"])
        for b in range(NSEQ):
            for kc in range(8):
                ts("dve", cb[b][:, kc, :], ones[:], cs[:, b * 8 + kc:b * 8 + kc + 1], None, ALU.mult, None,
                   ["ones", "cs"], [("cb", b)])
        P.flush()

    def phase_M(l, win):
        with ExitStack() as es:
            wm = sb("wm", [128, 8, 3 * D], BF16, es)
            bmod = sb("bmod", [128, 3 * D], F32, es)
            pre = sb("pre", [128, D], F32, es)
            post = sb("post", [128, D], F32, es)
            ps = pt("psM", [128, 6, 512], F32, es)
            dma("sp", bmod[:], b_mod[l:l + 1, :].partition_broadcast(128), [], ["bmod"], "bmod")
            dma("sp", pre[:], pre_norm[l:l + 1, :].partition_broadcast(128), [], ["pre"], "pre")
            dma("sp", post[:], post_norm[l:l + 1, :].partition_broadcast(128), [], ["post"], "post")
            for kc in range(8):
                dma("pool", wm[:, kc, :], w_mod[l, kc * 128:(kc + 1) * 128, :], [], [("wm", kc)], f"wm{kc % 2}")
            for kc in range(8):
                dma("pool", win[:, kc, :], w_in[l, kc * 128:(kc + 1) * 128, :], [], [("win", kc)], f"win{kc}")
            for b in range(NSEQ):
                items = [(ps[:, n, :], cb[b][:, kc, :], wm[:, kc, n * 512:(n + 1) * 512], kc == 0, kc == 7, False)
                         for n in range(6) for kc in range(8)]
                mmgroup(items, [("wm", kc) for kc in range(8)] + [("cb", b)], ["psM"])
                psv = ps[:, :, :].rearrange("p a b -> p (a b)")
                tt("dve", SH[b][:], psv[:, 0:D], bmod[:, 0:D], ALU.add, ["psM", "bmod"], [("SH", b)])
                tt("dve", AM[b][:], psv[:, D:2 * D], bmod[:, D:2 * D], ALU.add, ["psM", "bmod"], [("AM", b)])
                tt("dve", GM[b][:], psv[:, 2 * D:3 * D], bmod[:, 2 * D:3 * D], ALU.add, ["psM", "bmod"], [("GM", b)])
                stt("dve", AM[b][:], AM[b][:], 1.0, pre[:], ALU.add, ALU.mult, [("AM", b), "pre"], [("AM", b)])
                tt("pool", GM[b][:], GM[b][:], post[:], ALU.mult, [("GM", b), "post"], [("GM", b)])
            P.flush()

    def phase_A(l, xsrc, qscale, win):
        with ExitStack() as es:
            xt = [sb(f"xtA{i}", [128, D], F32, es) for i in range(4)]
            junk = sb("junkA", [128, D], BF16, es)
            ssq = [sb(f"ssqA{i}", [128, 1], F32, es) for i in range(4)]
            rs = [sb(f"rsA{i}", [128, 1], F32, es) for i in range(4)]
            t1 = [sb(f"t1A{i}", [128, D], F32, es) for i in range(2)]
            hb = [sb(f"hbA{i}", [128, D], BF16, es) for i in range(4)]
            hT = [sb(f"hTA{i}", [128, 8, 512], BF16, es) for i in range(2)]
            stf = [sb(f"stfA{i}", [128, 512], BF16, es) for i in range(4)]
            stv = [sb(f"stvA{i}", [128, DI], BF16, es) for i in range(2)]
            pst = pt("pstA", [128, 2, 1024], BF16, es)
            ps = pt("psA", [128, 6, 512], F32, es)
            winkeys = [("win", kc) for kc in range(8)]
            cnt = dict(bank=0, nst=0, nsv=0)

            def prepA(g):
                Ts = [g * 4 + t4 for t4 in range(4)]
                for T in Ts:
                    i4 = T % 4
                    dma("pool", xt[i4][:], xsrc[T * 128:(T + 1) * 128, :], [], [("xt", i4)], f"xtA{i4}")
                for T in Ts:
                    i4 = T % 4
                    act(junk[:], xt[i4][:], AF.Square, [("xt", i4)], ["junk", ("ssq", i4)], accum_out=ssq[i4][:])
                for T in Ts:
                    i4 = T % 4
                    ts("dve", rs[i4][:], ssq[i4][:], 1.0 / D, RMS_EPS, ALU.mult, ALU.add, [("ssq", i4)], [("rs", i4)])
                for T in Ts:
                    i4 = T % 4
                    act(rs[i4][:], rs[i4][:], AF.Ln, [("rs", i4)], [("rs", i4)])
                for T in Ts:
                    i4 = T % 4
                    act(rs[i4][:], rs[i4][:], AF.Exp, [("rs", i4)], [("rs", i4)], scale=-0.5)
                for T in Ts:
                    b = (T * 128) // S
                    i2 = T % 2
                    i4 = T % 4
                    stt("dve", t1[i2][:], xt[i4][:], rs[i4][:, 0:1], AM[b][:], ALU.mult, ALU.mult,
                        [("xt", i4), ("rs", i4), ("AM", b)], [("t1", i2)])
                    tt("pool", hb[i4][:], t1[i2][:], SH[b][:], ALU.add, [("t1", i2), ("SH", b)], [("hb", i4)])

            def prepB(g):
                hk = ("hT", g % 2)
                for t4 in range(4):
                    T = g * 4 + t4
                    i2 = T % 2
                    i4 = T % 4
                    trgroup([(pst[:, i2, kc * 128:(kc + 1) * 128], hb[i4][:, kc * 128:(kc + 1) * 128]) for kc in range(8)],
                            [("hb", i4)], [("pst", i2)])
                    act(hT[g % 2][:, :, t4 * 128:(t4 + 1) * 128], pst[:, i2, :].rearrange("p (a b) -> p a b", a=8),
                        AF.Copy, [("pst", i2)], [("hT", g % 2, t4)])

            def projF(g):
                hk = ("hT", g % 2)
                fm = [(qT, n, n * 128, "q") for n in range(8)] + [(kT, n, D + n * 128, "k") for n in range(8)] + \
                     [(gT, n, 2 * D + DI + n * 128, "g") for n in range(16)]
                for (dst, n, col, kind) in fm:
                    bk = cnt["bank"] % 6
                    cnt["bank"] += 1
                    items = [(ps[:, bk, :], win[:, kc, col:col + 128], hT[g % 2][:, kc, :], kc == 0, kc == 7, False)
                             for kc in range(8)]
                    mmgroup(items, winkeys + [("hT", g % 2, q4) for q4 in range(4)], [("ps", bk)])
                    si = cnt["nst"] % 4
                    cnt["nst"] += 1
                    if kind == "q":
                        ts("dve", stf[si][:], ps[:, bk, :], float(qscale), None, ALU.mult, None, [("ps", bk)], [("stf", si)])
                    elif kind == "k":
                        P.add("dve", lambda g_, si=si, bk=bk: g_.tensor_copy(out=stf[si][:], in_=ps[:, bk, :]),
                              reads=[("ps", bk)], writes=[("stf", si)])
                    else:
                        act(stf[si][:], ps[:, bk, :], AF.Silu, [("ps", bk)], [("stf", si)])
                    dma("sp", dst[n * 128:(n + 1) * 128, g * 512:(g + 1) * 512], stf[si][:], [("stf", si)], [], f"stfA{si}")

            def projV(g):
                hk = ("hT", g % 2)
                for t4 in range(4):
                    T = g * 4 + t4
                    sv = cnt["nsv"] % 2
                    cnt["nsv"] += 1
                    for nv in range(4):
                        bk = cnt["bank"] % 6
                        cnt["bank"] += 1
                        col = 2 * D + nv * 512
                        items = [(ps[:, bk, :], hT[g % 2][:, kc, t4 * 128:(t4 + 1) * 128], win[:, kc, col:col + 512],
                                  kc == 0, kc == 7, False) for kc in range(8)]
                        mmgroup(items, winkeys + [("hT", g % 2, t4)], [("ps", bk)])
                        if nv % 2 == 0:
                            P.add("dve", lambda g_, sv=sv, bk=bk, nv=nv: g_.tensor_copy(
                                out=stv[sv][:, nv * 512:(nv + 1) * 512], in_=ps[:, bk, :]),
                                reads=[("ps", bk)], writes=[("stv", sv)])
                        else:
                            act(stv[sv][:, nv * 512:(nv + 1) * 512], ps[:, bk, :], AF.Copy, [("ps", bk)], [("stv", sv)])
                    dma("sp", vS[T * 128:(T + 1) * 128, :], stv[sv][:], [("stv", sv)], [], f"stvA{sv}")

            prepA(0)
            prepB(0)
            for g in range(NG):
                if g + 1 < NG:
                    prepA(g + 1)
                projF(g)
                if g + 1 < NG:
                    prepB(g + 1)
                projV(g)
            P.flush()

    def phase_B_sb(l, wo_pre=None):
        QW = 1024
        NQC = S // QW
        KPC = QW // 128
        with ExitStack() as es:
            qh = [sb(f"qh{i}", [128, S], BF16, es) for i in range(2)]
            kh = [sb(f"kh{i}", [128, S], BF16, es) for i in range(2)]
            vh = [sb(f"vh{i}", [128, TPS, 128], BF16, es) for i in range(2)]
            sgh = [sb(f"sgh{i}", [128, S], BF16, es) for i in range(2)]
            e_sb = [sb(f"e_sb{i}", [128, QW], F32, es) for i in range(4)]
            sp_sb = [sb(f"sp_sb{i}", [128, QW], BF16, es) for i in range(3)]
            p_sb = [sb(f"p_sb{i}", [128, QW], F32, es) for i in range(2)]
            a_sb = [sb(f"a_sb{i}", [128, QW], BF16, es) for i in range(3)]
            ssb = [sb(f"ssb{i}", [128, QW], BF16, es) for i in range(2)]
            ogs = [sb(f"ogsB{i}", [128, QW], BF16, es) for i in range(2)]
            pz = pt("pzB", [128, 4, 512], F32, es)
            pw = pt("pwB", [128, 2, 512], F32, es)
            po = pt("poB", [128, 2, 512], F32, es)
            pwf = pw[:, :, :].rearrange("p a b -> p (a b)")
            pof = po[:, :, :].rearrange("p a b -> p (a b)")
            if wo_pre is not None:
                for kc in range(16):
                    dma("pool", wo_pre[:, kc, :], w_out[l, kc * 128:(kc + 1) * 128, :], [], [("wo", kc)], f"wo{kc % 4}")
            for i in range(2):
                P.add("dve", lambda g, i=i: g.memset(qh[i][64:128, :], 0.0), writes=[("qh", i)])
                P.add("pool", lambda g, i=i: g.memset(kh[i][64:128, :], 0.0), writes=[("kh", i)])
            tiles = []
            hc = 0
            cc = 0
            for b in range(NSEQ):
                for h in range(16):
                    for qc in range(NQC):
                        tp = KPC * qc + KPC - 1
                        for kb in range(tp, -1, -1):
                            tiles.append(dict(b=b, h=h, qc=qc, kb=kb, j=tp - kb, top=tp, hc=hc, cc=cc,
                                              first=(qc == 0 and kb == tp)))
                        cc += 1
                    hc += 1
            n = len(tiles)

            def geom(t):
                kb, qc = t["kb"], t["qc"]
                diag = kb >= KPC * qc
                c0 = 128 * (kb - KPC * qc) if diag else 0
                segs = []
                if c0 < 512:
                    segs.append((0, c0, 512))
                segs.append((1, max(c0, 512), QW))
                return diag, c0, segs

            def load_head(hc_):
                if hc_ >= NSEQ * 16:
                    return
                b_, h_ = hc_ // 16, hc_ % 16
                p2 = hc_ % 2
                tok = slice(b_ * S, (b_ + 1) * S)
                dma("sp", qh[p2][0:64, :], qT[h_ * 64:(h_ + 1) * 64, tok], [], [("qh", p2)], f"qh{p2}")
                dma("sp", kh[p2][0:64, :], kT[h_ * 64:(h_ + 1) * 64, tok], [], [("kh", p2)], f"kh{p2}")
                dma("sp", vh[p2][:], vS[tok, h_ * 128:(h_ + 1) * 128].rearrange("(kb p) d -> p kb d", p=128),
                    [], [("vh", p2)], f"vh{p2}")
                dma("sp", sgh[p2][:], gT[h_ * 128:(h_ + 1) * 128, tok], [], [("sgh", p2)], f"sgh{p2}")

            def prefetch(i):
                t = tiles[i]
                if t["qc"] == 0 and t["kb"] == 0:
                    load_head(t["hc"] + 1)

            def st1(i):
                t = tiles[i]
                qc, kb = t["qc"], t["kb"]
                s2 = t["hc"] % 2
                if t["first"] and t["hc"] == 0:
                    load_head(0)
                diag, c0, segs = geom(t)
                t0 = qc * QW
                pp = i % 2
                items = []
                rd = [("qh", s2), ("kh", s2)]
                for (bk, lo, hi) in segs:
                    has_mask = diag and lo <= c0 < hi
                    items.append((pz[:, 2 * pp + bk, lo - 512 * bk:hi - 512 * bk], kh[s2][:, kb * 128:(kb + 1) * 128],
                                  qh[s2][:, t0 + lo:t0 + hi], True, not has_mask, False))
                    if has_mask:
                        items.append((pz[:, 2 * pp + bk, c0 - 512 * bk:c0 - 512 * bk + 128], ident[:], negmask[:],
                                      False, True, True))
                if diag:
                    rd += ["ident", "negmask"]
                mmgroup(items, rd, [("pz", pp)])
                pzf = pz[:, 2 * pp:2 * pp + 2, :].rearrange("p a b -> p (a b)")
                act(e_sb[i % 4][:, c0:QW], pzf[:, c0:QW], AF.Exp, [("pz", pp)], [("e", i % 4)])

            def st1b(i):
                t = tiles[i]
                diag, c0, segs = geom(t)
                act(sp_sb[i % 3][:, c0:QW], e_sb[i % 4][:, c0:QW], AF.Ln, [("e", i % 4)], [("sp", i % 3)], bias=1.0)

            def st2(i):
                t = tiles[i]
                kb, j = t["kb"], t["j"]
                diag, c0, segs = geom(t)
                y = i % 3
                w2 = i % 2
                items = []
                rd = [("sp", y), "lneg"]
                for (bk, lo, hi) in segs:
                    items.append((pw[:, bk, lo - 512 * bk:hi - 512 * bk], lneg[:], sp_sb[y][:, lo:hi], True, j == 0, False))
                    if j > 0:
                        items.append((pw[:, bk, lo - 512 * bk:hi - 512 * bk], oneg[:], ssb[j % 2][:, lo:hi], False, True, False))
                if j > 0:
                    rd += ["oneg", ("ss", j % 2), ("ssz", j % 2)]
                mmgroup(items, rd, ["pw"])
                if kb > 0:
                    nx = (j + 1) % 2
                    e = "pool" if i % 2 == 0 else "dve"
                    if c0 > 0:
                        P.add(e, lambda g: g.memset(ssb[nx][:, 0:c0], 0.0), writes=[("ssz", nx)])
                    if j == 0:
                        P.add(e, lambda g: g.tensor_copy(out=ssb[nx][:, c0:QW], in_=sp_sb[y][:, c0:QW]),
                              reads=[("sp", y)], writes=[("ss", nx)])
                    else:
                        tt(e, ssb[nx][:, c0:QW], ssb[j % 2][:, c0:QW], sp_sb[y][:, c0:QW], ALU.add,
                           [("ss", j % 2), ("sp", y)], [("ss", nx)])
                act(p_sb[w2][:, c0:QW], pwf[:, c0:QW], AF.Exp, ["pw"], [("p", w2)])

            def st3(i):
                t = tiles[i]
                diag, c0, segs = geom(t)
                tt("dve", a_sb[i % 3][:, c0:QW], e_sb[i % 4][:, c0:QW], p_sb[i % 2][:, c0:QW], ALU.mult,
                   [("e", i % 4), ("p", i % 2)], [("a", i % 3)])

            def st4(i):
                t = tiles[i]
                b, h, qc, kb, j = t["b"], t["h"], t["qc"], t["kb"], t["j"]
                s2 = t["hc"] % 2
                diag, c0, segs = geom(t)
                items = []
                for (bk, lo, hi) in segs:
                    first = (j == 0) if bk == 1 else (j == KPC // 2)
                    items.append((po[:, bk, lo - 512 * bk:hi - 512 * bk], vh[s2][:, kb, :], a_sb[i % 3][:, lo:hi],
                                  first, kb == 0, True))
                mmgroup(items, [("vh", s2), ("a", i % 3)], ["po"])
                if kb == 0:
                    t0 = qc * QW
                    o2 = t["cc"] % 2
                    tt("dve", ogs[o2][:], pof, sgh[s2][:, t0:t0 + QW], ALU.mult, ["po", ("sgh", s2)], [("ogs", o2)])
                    dma("sp", ogT[h * 128:(h + 1) * 128, b * S + t0:b * S + t0 + QW], ogs[o2][:],
                        [("ogs", o2)], [], f"ogsB{o2}")

            for step in range(n + 4):
                if step < n:
                    st1(step)
                if 0 <= step - 1 < n:
                    st1b(step - 1)
                if 0 <= step - 2 < n:
                    st2(step - 2)
                if 0 <= step - 3 < n:
                    st3(step - 3)
                if 0 <= step - 4 < n:
                    st4(step - 4)
                if step < n:
                    prefetch(step)
            P.flush()

    def phase_B_ret(l):
        gi = l // 2
        GT = 256
        CH = GT // 128
        NGR = S // GT
        with ExitStack() as es:
            qg = [sb(f"qgR{i}", [128, 8, GT], BF16, es) for i in range(2)]
            kg = [sb(f"kgR{i}", [128, 8, GT], BF16, es) for i in range(2)]
            vg = [sb(f"vgR{i}", [128, CH, DI], BF16, es) for i in range(2)]
            sgg = [sb(f"sggR{i}", [128, 16, GT], BF16, es) for i in range(2)]
            cs_ = [sb(f"cosR{i}", [128, GT], F32, es) for i in range(2)]
            sn_ = [sb(f"sinR{i}", [128, GT], F32, es) for i in range(2)]
            ta = {(e, i): sb(f"taR{e}{i}", [128, GT], F32, es) for e in ("dve", "pool") for i in range(2)}
            tb = {(e, i): sb(f"tbR{e}{i}", [128, GT], F32, es) for e in ("dve", "pool") for i in range(2)}
            qr = [sb(f"qrR{i}", [128, 8, GT], BF16, es) for i in range(2)]
            kr = [sb(f"krR{i}", [128, 8, GT], BF16, es) for i in range(2)]
            ktok = sb("ktokR", [128, CH, D], BF16, es)
            st = [sb(f"stR{h}", [128, 2, 512], F32, es) for h in range(4)]
            stb = [sb(f"stbR{h}", [128, 2, 512], BF16, es) for h in range(4)]
            scm = [sb(f"scmR{i}", [128, 4, 128], BF16, es) for i in range(2)]
            gncol = sb("gncolR", [128, 16], F32, es)
            bst = [sb(f"bstR{i}", [128, 6], F32, es) for i in range(4)]
            mv = [sb(f"mvR{i}", [128, 2], F32, es) for i in range(4)]
            rsg = [sb(f"rsgR{i}", [128, 1], F32, es) for i in range(4)]
            ob = [sb(f"obR{i}", [128, DI], BF16, es) for i in range(2)]
            ogs = sb("ogsR", [128, 16, GT], BF16, es)
            psf = pt("psfR", [128, 6, 512], F32, es)
            psb = pt("psbR", [128, 2, 1024], BF16, es)
            dma("sp", gncol[:], gncol_d[gi], [], ["gncol"], "gnb")
            groups = [(b, g4) for b in range(NSEQ) for g4 in range(NGR)]
            NGRP = len(groups)

            def load(gx):
                b, g4 = groups[gx]
                z = gx % 2
                p0 = g4 * GT
                tk = slice(b * S + p0, b * S + p0 + GT)
                dma("sp", qg[z][:], qT[:, tk].rearrange("(fc p) t -> p fc t", p=128), [], [("qg", z)], f"qgR{z}")
                dma("sp", kg[z][:], kT[:, tk].rearrange("(fc p) t -> p fc t", p=128), [], [("kg", z)], f"kgR{z}")
                dma("sp", cs_[z][:], cos_d[:, p0:p0 + GT], [], [("cos", z)], f"cosR{z}")
                dma("sp", sn_[z][:], sin_d[:, p0:p0 + GT], [], [("sin", z)], f"sinR{z}")
                dma("sp", vg[z][:], vS[tk, :].rearrange("(tt p) d -> p tt d", p=128), [], [("vg", z)], f"vgR{z}")
                dma("sp", sgg[z][:], gT[:, tk].rearrange("(fc p) t -> p fc t", p=128), [], [("sgg", z)], f"sggR{z}")

            def rope_units(gx):
                z = gx % 2
                units = []
                k_ = 0
                ecnt = {}
                for h in range(4):
                    for (src, skey, dst, dkey, fac, fkey) in ((qg[z], ("qg", z), qr[z], "qr", qfac, "qfac"),
                                                              (kg[z], ("kg", z), kr[z], "kr", kfac, "kfac")):
                        for half in range(2):
                            en_ = "dve" if k_ % 5 == 0 else "pool"
                            ecnt[en_] = ecnt.get(en_, 0) + 1
                            e = (en_, ecnt[en_] % 2)
                            k_ += 1

                            def unit(e=e, h=h, src=src, skey=skey, dst=dst, dkey=dkey, fac=fac, fkey=fkey, half=half):
                                fbc = fac[:, h * 128:(h + 1) * 128].unsqueeze(1).broadcast_to([128, CH, 128])
                                xa = src[:, 2 * h + half, :]
                                xb_ = src[:, 2 * h + 1 - half, :]
                                tak, tbk = ("ta", e), ("tb", e)
                                en = e[0]
                                tt(en, ta[e][:], xa, cs_[z][:], ALU.mult, [skey, ("cos", z)], [tak])
                                tt(en, tb[e][:], xb_, sn_[z][:], ALU.mult, [skey, ("sin", z)], [tbk])
                                tt(en, ta[e][:], ta[e][:], tb[e][:], ALU.subtract if half == 0 else ALU.add,
                                   [tak, tbk], [tak])
                                tt(en, dst[:, 2 * h + half, :].rearrange("p (a b) -> p a b", a=CH),
                                   ta[e][:].rearrange("p (a b) -> p a b", a=CH), fbc, ALU.mult,
                                   [tak, fkey], [(dkey, z, 2 * h + half)])
                            units.append(unit)
                return units

            cnts = dict(o=0, c=0)

            def gnfold(gx):
                z = gx % 2
                for fc in range(16):
                    tt("pool", sgg[z][:, fc, :], sgg[z][:, fc, :], gncol[:, fc:fc + 1].broadcast_to([128, GT]), ALU.mult,
                       [("sgg", z), "gncol"], [("sggf", z, fc)])

            def chunks(gx, pending):
                b, g4 = groups[gx]
                z = gx % 2
                p0 = g4 * GT
                tk = slice(b * S + p0, b * S + p0 + GT)
                if g4 == 0:
                    for h in range(4):
                        P.add("pool", lambda g, h=h: g.memset(st[h][:], 0.0), writes=[("st", h)])
                        P.add("pool", lambda g, h=h: g.memset(stb[h][:], 0.0), writes=[("stb", h)])
                if gx == 0:
                    gnfold(0)
                if gx + 1 < NGRP:
                    gnfold(gx + 1)
                krkeys = [("kr", z, i) for i in range(8)]
                qrkeys = [("qr", z, i) for i in range(8)]
                for c in range(CH):
                    tsl = slice(c * 128, (c + 1) * 128)
                    trgroup([(psb[:, 0, fc * 128:(fc + 1) * 128], kr[z][:, fc, tsl]) for fc in range(8)],
                            krkeys, [("psb", 0)])
                    for h in range(4):
                        act(ktok[:, c, h * 256:(h + 1) * 256], psb[:, 0, h * 256:(h + 1) * 256], AF.Copy,
                            [("psb", 0)], [("ktok", c, h)], scale=cdec[h])
                for c in range(CH):
                    tsl = slice(c * 128, (c + 1) * 128)
                    items = []
                    for h in range(4):
                        for half in range(2):
                            items.append((psf[:, 0, h * 128:(h + 1) * 128], kr[z][:, 2 * h + half, tsl],
                                          qr[z][:, 2 * h + half, tsl], half == 0, half == 1, True))
                    mmgroup(items, krkeys + qrkeys, [("psf", 0)])
                    cc = cnts["c"]
                    cnts["c"] += 1
                    sc = scm[cc % 2]
                    sck = ("scm", cc % 2)
                    tt("dve", sc[:], psf[:, 0, :].rearrange("p (a b) -> p a b", a=4),
                       caus[:].unsqueeze(1).broadcast_to([128, 4, 128]), ALU.mult, [("psf", 0), "caus"], [sck])
                    obk = ("ob", cc % 2)
                    obt = ob[cc % 2]
                    o2s = {}

                    def o_mm(h):
                        o2 = 1 + cnts["o"] % 3
                        cnts["o"] += 1
                        o2s[h] = o2
                        vsl = vg[z][:, c, h * 512:(h + 1) * 512]
                        mmgroup([(psf[:, o2, :], sc[:, h, :], vsl, True, False, False),
                                 (psf[:, o2, :], qr[z][:, 2 * h, tsl], stb[h][:, 0, :], False, False, False),
                                 (psf[:, o2, :], qr[z][:, 2 * h + 1, tsl], stb[h][:, 1, :], False, True, False)],
                                [sck, ("vg", z), ("qr", z, 2 * h), ("qr", z, 2 * h + 1), ("stb", h)], [("psf", o2)])

                    def state_upd(h):
                        vsl = vg[z][:, c, h * 512:(h + 1) * 512]
                        mmgroup([(psf[:, 4 + half, :], ktok[:, c, h * 256 + half * 128:h * 256 + (half + 1) * 128],
                                  vsl, True, True, False) for half in range(2)],
                                [("ktok", c, h), ("vg", z)], [("psf", 4)])
                        stt("dve", st[h][:].rearrange("p a b -> p (a b)"), st[h][:].rearrange("p a b -> p (a b)"),
                            cdec[h], psf[:, 4:6, :].rearrange("p a b -> p (a b)"), ALU.mult, ALU.add,
                            [("st", h), ("psf", 4)], [("st", h)])
                        act(stb[h][:], st[h][:], AF.Copy, [("st", h)], [("stb", h)])

                    def gn_stats(h):
                        o2 = o2s[h]
                        P.add("dve", lambda g, h=h, o2=o2: g.bn_stats(out=bst[h][:], in_=psf[:, o2, :]),
                              reads=[("psf", o2)], writes=[("bst", h)])
                        P.add("dve", lambda g, h=h: g.bn_aggr(out=mv[h][:], in_=bst[h][:]),
                              reads=[("bst", h)], writes=[("mv", h)])
                        ts("dve", rsg[h][:], mv[h][:, 1:2], 1.0, GN_EPS, ALU.mult, ALU.add, [("mv", h)], [("rsg", h)])
                        act(rsg[h][:], rsg[h][:], AF.Ln, [("rsg", h)], [("rsg", h)])
                        act(rsg[h][:], rsg[h][:], AF.Exp, [("rsg", h)], [("rsg", h)], scale=-0.5)

                    def gn_norm(h):
                        o2 = o2s[h]
                        ts("dve", obt[:, h * 512:(h + 1) * 512], psf[:, o2, :], mv[h][:, 0:1], rsg[h][:, 0:1],
                           ALU.subtract, ALU.mult, [("psf", o2), ("mv", h), ("rsg", h)], [obk])

                    def fill(k):
                        for _ in range(k):
                            if pending:
                                pending.pop(0)()

                    o_mm(0)
                    o_mm(1)
                    o_mm(2)
                    state_upd(0)
                    gn_stats(0)
                    state_upd(1)
                    gn_stats(1)
                    gn_norm(0)
                    o_mm(3)
                    state_upd(2)
                    gn_stats(2)
                    gn_norm(1)
                    state_upd(3)
                    gn_stats(3)
                    gn_norm(2)
                    fill(16 // CH)
                    gn_norm(3)
                    for hf in range(2):
                        trgroup([(psb[:, hf, fc * 128:(fc + 1) * 128], obt[:, (hf * 8 + fc) * 128:(hf * 8 + fc + 1) * 128])
                                 for fc in range(8)], [obk], [("psb", hf)])
                        tt("dve", ogs[:, hf * 8:(hf + 1) * 8, tsl], psb[:, hf, :].rearrange("p (a b) -> p a b", a=8),
                           sgg[z][:, hf * 8:(hf + 1) * 8, tsl], ALU.mult,
                           [("psb", hf), ("sgg", z)] + [("sggf", z, hf * 8 + q) for q in range(8)], ["ogs"])
                while pending:
                    pending.pop(0)()
                dma("sp", ogT[:, tk].rearrange("(fc p) t -> p fc t", p=128), ogs[:], ["ogs"], [], "ogsR")

            load(0)
            if NGRP > 1:
                load(1)
            for u in rope_units(0):
                u()
            for gx in range(NGRP):
                pending = rope_units(gx + 1) if gx + 1 < NGRP else []
                chunks(gx, pending)
                if gx + 2 < NGRP:
                    load(gx + 2)
            P.flush()

    def phase_C(l, xsrc, wo_pre=None):
        with ExitStack() as es:
            wo = wo_pre if wo_pre is not None else sb("woC", [128, 16, D], BF16, es)
            ogg = [sb(f"oggC{i}", [128, 16, 512], BF16, es) for i in range(2)]
            xt = [sb(f"xtC{i}", [128, D], F32, es) for i in range(2)]
            junk = sb("junkC", [128, D], BF16, es)
            ssq = [sb(f"ssqC{i}", [128, 1], F32, es) for i in range(2)]
            rs = [sb(f"rsC{i}", [128, 1], F32, es) for i in range(2)]
            t1 = [sb(f"t1C{i}", [128, D], F32, es) for i in range(2)]
            xo = [sb(f"xoC{i}", [128, D], F32, es) for i in range(2)]
            py = pt("pyC", [128, 4, 512], F32, es)
            if wo_pre is None:
                for kc in range(16):
                    dma("pool", wo[:, kc, :], w_out[l, kc * 128:(kc + 1) * 128, :], [], [("wo", kc)], f"wo{kc % 4}")
            wokeys = [("wo", kc) for kc in range(16)]
            def ld_ogg(g):
                dma("sp", ogg[g % 2][:], ogT[:, g * 512:(g + 1) * 512].rearrange("(kc p) t -> p kc t", p=128),
                    [], [("ogg", g % 2)], f"oggC{g % 2}")

            def ld_xt(T):
                dma("sp", xt[T % 2][:], xsrc[T * 128:(T + 1) * 128, :], [], [("xt", T % 2)], f"xtC{T % 2}")

            ld_ogg(0)
            ld_xt(0)
            for g in range(NG):
                g2 = g % 2
                if g + 1 < NG:
                    ld_ogg(g + 1)
                for t4 in range(4):
                    T = g * 4 + t4
                    b = (T * 128) // S
                    i2 = T % 2
                    if T + 1 < NT:
                        ld_xt(T + 1)
                    items = []
                    for hf in range(2):
                        for kc in range(16):
                            items.append((py[:, 2 * i2 + hf, :], ogg[g2][:, kc, t4 * 128:(t4 + 1) * 128],
                                          wo[:, kc, hf * 512:(hf + 1) * 512], kc == 0, kc == 15, False))
                    mmgroup(items, wokeys + [("ogg", g2)], [("py", i2)])
                    yv = py[:, 2 * i2:2 * i2 + 2, :].rearrange("p a b -> p (a b)")
                    act(junk[:], yv, AF.Square, [("py", i2)], ["junk", ("ssq", i2)], accum_out=ssq[i2][:])
                    rstd_chain(ssq[i2][:], rs[i2][:], ("ssq", i2), ("rs", i2), 1.0 / D, RMS_EPS)
                    stt("dve", t1[i2][:], yv, rs[i2][:, 0:1], GM[b][:], ALU.mult, ALU.mult,
                        [("py", i2), ("rs", i2), ("GM", b)], [("t1", i2)])
                    tt("pool", xo[i2][:], t1[i2][:], xt[i2][:], ALU.add, [("t1", i2), ("xt", i2)], [("xo", i2)])
                    dma("sp", out[T * 128:(T + 1) * 128, :], xo[i2][:], [("xo", i2)], [], f"xoC{i2}")
            P.flush()

    first = True
    for l in layers:
        xsrc = x_in if first else out
        first = False
        with ExitStack() as esw:
            win_l = sb("win", [128, 8, DP], BF16, esw)
            phase_M(l, win_l)
            phase_A(l, xsrc, 1.0 if l % 2 == 0 else 0.125, win_l)
        if l % 2 == 0:
            phase_B_ret(l)
            phase_C(l, xsrc)
        else:
            with ExitStack() as es2:
                wo_pre = sb("woP", [128, 16, D], BF16, es2)
                phase_B_sb(l, wo_pre)
                phase_C(l, xsrc, wo_pre)
    P.add("sp", lambda g: g.nop(), reads=[], writes=[])
    P.flush(final=True)
    top.close()
    return nc, P


_CACHE = {}


def _get_program():
    if "nc" not in _CACHE:
        _CACHE["nc"] = build_program(SEQ, BATCH // NCORES, list(range(DEPTH)))[0]
    return _CACHE["nc"]


def make_in_maps(x, c, w_in, w_out, w_mod, b_mod, pre_norm, post_norm, ret_gn, S, nseq, ncores):
    consts, _ = make_consts(S)
    f = lambda a: np.ascontiguousarray(np.asarray(a, dtype=np.float32))
    shared = {"w_in": f(w_in), "w_out": f(w_out), "w_mod": f(w_mod), "b_mod": f(b_mod),
              "pre_norm": f(pre_norm), "post_norm": f(post_norm),
              "gn_col": np.ascontiguousarray(f(ret_gn).reshape(2, 16, 128).transpose(0, 2, 1))}
    shared.update(consts)
    x = f(x)
    c = f(c)
    maps = []
    for i in range(ncores):
        xs = x[i * nseq:(i + 1) * nseq].reshape(nseq * S, D)
        cc = c[i * nseq:(i + 1) * nseq]
        cT = np.ascontiguousarray(cc.reshape(nseq, 8, 128).transpose(2, 0, 1).reshape(128, nseq * 8))
        m = dict(shared)
        m["x"] = np.ascontiguousarray(xs)
        m["cT"] = cT
        maps.append(m)
    return maps


def kernel(x, c, w_in, w_out, w_mod, b_mod, pre_norm, post_norm, ret_gn):
    nseq = BATCH // NCORES
    nc = _get_program()
    maps = make_in_maps(x, c, w_in, w_out, w_mod, b_mod, pre_norm, post_norm, ret_gn, SEQ, nseq, NCORES)
    res = run_bass_kernel_spmd(nc, maps, core_ids=list(range(NCORES)))
    outs = [np.asarray(r["out"]).reshape(nseq, SEQ, D) for r in res.results]
    return np.concatenate(outs, axis=0).astype(np.float32)
```
